# Optimizing a Trainium2 kernel written in Bass

```python
import math
import jax, jax.numpy as jnp
from jax import lax
import numpy as np

D_MODEL = 1024
BATCH = 8
SEQ = 2048
DEPTH = 1

MEM_LEN = 256
HEAD_DIM = 64
ATTN_WIDTH = D_MODEL // 2
N_Q_HEADS = ATTN_WIDTH // HEAD_DIM
N_KV_HEADS = N_Q_HEADS // 4
Q_PER_KV = N_Q_HEADS // N_KV_HEADS
WINDOW = 128
BLOCK = 128
CONV_CH = D_MODEL // 4
CONV_WIDTH = 31
MEM_WIDTH = D_MODEL // 4
N_MEM_HEADS = MEM_WIDTH // HEAD_DIM
MIX_WIDTH = ATTN_WIDTH + CONV_CH + MEM_WIDTH
ROPE_THETA = 500000.0
ROPE_DIM = HEAD_DIM // 4
D_FF = 256 * math.ceil(8 * D_MODEL / 3 / 256)
EPS = 1e-6

Q_COLS = N_Q_HEADS * HEAD_DIM
KV_COLS = N_KV_HEADS * HEAD_DIM
GLU_COLS = 2 * CONV_CH
MQ_COLS = N_MEM_HEADS * HEAD_DIM
IN_COLS = Q_COLS + 2 * KV_COLS + GLU_COLS + MQ_COLS
SPLITS = [Q_COLS, Q_COLS + KV_COLS, Q_COLS + 2 * KV_COLS, Q_COLS + 2 * KV_COLS + GLU_COLS]

kernel_name = "hymba_conformer_swa_sink_memory_layer"


def rms_norm(x, g):
    xf = x.astype(jnp.float32)
    y = xf * lax.rsqrt(jnp.mean(xf * xf, axis=-1, keepdims=True) + EPS)
    return (y * g.astype(jnp.float32)).astype(x.dtype)


def layer_norm(x, g, b):
    xf = x.astype(jnp.float32)
    mu = jnp.mean(xf, axis=-1, keepdims=True)
    var = jnp.mean(jnp.square(xf - mu), axis=-1, keepdims=True)
    y = (xf - mu) * lax.rsqrt(var + EPS)
    return (y * g.astype(jnp.float32) + b.astype(jnp.float32)).astype(x.dtype)


def swiglu(x, w_gate, w_up, w_down):
    return (jax.nn.silu(x @ w_gate) * (x @ w_up)) @ w_down


def rope_partial(x, positions):
    half = ROPE_DIM // 2
    inv_freq = ROPE_THETA ** (-jnp.arange(half, dtype=jnp.float32) / half)
    ang = positions.astype(jnp.float32)[..., None] * inv_freq
    cos = jnp.cos(ang)[:, :, None, :]
    sin = jnp.sin(ang)[:, :, None, :]
    xr = x[..., :ROPE_DIM].astype(jnp.float32)
    x1, x2 = xr[..., :half], xr[..., half:]
    rot = jnp.concatenate([x1 * cos - x2 * sin, x2 * cos + x1 * sin], axis=-1).astype(x.dtype)
    return jnp.concatenate([rot, x[..., ROPE_DIM:]], axis=-1)


def sliding_window_attention(q, k, v, sinks):
    B, T = q.shape[0], q.shape[1]
    nb = T // BLOCK
    qb = q.reshape(B, nb, BLOCK, N_KV_HEADS, Q_PER_KV, HEAD_DIM)

    def with_prev(a):
        a = a.reshape(B, nb, BLOCK, N_KV_HEADS, HEAD_DIM)
        prev = jnp.pad(a, ((0, 0), (1, 0), (0, 0), (0, 0), (0, 0)))[:, :-1]
        return jnp.concatenate([prev, a], axis=2)

    kb, vb = with_prev(k), with_prev(v)
    s = jnp.einsum('bnqgrd,bnkgd->bngrqk', qb, kb).astype(jnp.float32) * (HEAD_DIM ** -0.5)
    qi = jnp.arange(BLOCK)[:, None] + BLOCK
    ki = jnp.arange(2 * BLOCK)[None, :]
    rel = qi - ki
    band = (rel >= 0) & (rel < WINDOW)
    blk = jnp.arange(nb)[:, None, None]
    valid = band[None] & ((blk > 0) | (ki >= BLOCK)[None])
    s = jnp.where(valid[None, :, None, None], s, -jnp.inf)
    sink = sinks.astype(jnp.float32).reshape(N_KV_HEADS, Q_PER_KV)[None, None, :, :, None, None]
    m = jnp.maximum(jnp.max(s, axis=-1, keepdims=True), sink)
    p = jnp.exp(s - m)
    denom = jnp.sum(p, axis=-1, keepdims=True) + jnp.exp(sink - m)
    w = (p / denom).astype(v.dtype)
    o = jnp.einsum('bngrqk,bnkgd->bnqgrd', w, vb)
    return o.reshape(B, T, N_Q_HEADS * HEAD_DIM)


def memory_cross_attention(q, k, v):
    B, T = q.shape[0], q.shape[1]
    s = jnp.einsum('bthd,bmhd->bhtm', q, k).astype(jnp.float32) * (HEAD_DIM ** -0.5)
    w = jax.nn.softmax(s, axis=-1).astype(v.dtype)
    o = jnp.einsum('bhtm,bmhd->bthd', w, v)
    return o.reshape(B, T, N_MEM_HEADS * HEAD_DIM)


def conformer_conv(u, w_dw, b_dw, g_ln, b_ln):
    a, gate = jnp.split(u, 2, axis=-1)
    h = a * jax.nn.sigmoid(gate)
    h = lax.conv_general_dilated(
        h, w_dw[:, None, :].astype(h.dtype), window_strides=(1,),
        padding=((CONV_WIDTH - 1, 0),), dimension_numbers=('NWC', 'WIO', 'NWC'),
        feature_group_count=CONV_CH) + b_dw
    return jax.nn.silu(layer_norm(h, g_ln, b_ln))


def setup_inputs(seed: int = 0) -> dict:
    key = jax.random.key(seed)
    ks = jax.random.split(key, 32)
    f32 = jnp.float32

    def w(k, shape, fan_in):
        return jax.random.normal(k, shape, f32) * (fan_in ** -0.5)

    def gain(k, shape):
        return 1.0 + 0.05 * jax.random.normal(k, shape, f32)

    L = DEPTH
    start = jax.random.randint(ks[2], (BATCH, 1), 0, 1024, dtype=jnp.int32)
    positions = start + jnp.arange(SEQ, dtype=jnp.int32)[None, :]
    return {
        "x": jax.random.normal(ks[0], (BATCH, SEQ, D_MODEL), f32),
        "mem": jax.random.normal(ks[1], (BATCH, MEM_LEN, D_MODEL), f32),
        "positions": positions,
        "g_ffn1": gain(ks[3], (L, D_MODEL)),
        "w_ffn1_gate": w(ks[4], (L, D_MODEL, D_FF), D_MODEL),
        "w_ffn1_up": w(ks[5], (L, D_MODEL, D_FF), D_MODEL),
        "w_ffn1_down": w(ks[6], (L, D_FF, D_MODEL), D_FF),
        "g_mix": gain(ks[7], (L, D_MODEL)),
        "w_in": w(ks[8], (L, D_MODEL, IN_COLS), D_MODEL),
        "g_q": gain(ks[9], (L, HEAD_DIM)),
        "g_k": gain(ks[10], (L, HEAD_DIM)),
        "sinks": 0.5 * jax.random.normal(ks[11], (L, N_Q_HEADS), f32),
        "w_dw": w(ks[12], (L, CONV_WIDTH, CONV_CH), CONV_WIDTH),
        "b_dw": 0.02 * jax.random.normal(ks[13], (L, CONV_CH), f32),
        "g_conv_ln": gain(ks[14], (L, CONV_CH)),
        "b_conv_ln": 0.02 * jax.random.normal(ks[15], (L, CONV_CH), f32),
        "g_mem": gain(ks[16], (L, D_MODEL)),
        "w_mem_kv": w(ks[17], (L, D_MODEL, 2 * MQ_COLS), D_MODEL),
        "g_mq": gain(ks[18], (L, HEAD_DIM)),
        "g_mk": gain(ks[19], (L, HEAD_DIM)),
        "w_out": w(ks[20], (L, MIX_WIDTH, D_MODEL), MIX_WIDTH),
        "g_ffn2": gain(ks[21], (L, D_MODEL)),
        "w_ffn2_gate": w(ks[22], (L, D_MODEL, D_FF), D_MODEL),
        "w_ffn2_up": w(ks[23], (L, D_MODEL, D_FF), D_MODEL),
        "w_ffn2_down": w(ks[24], (L, D_FF, D_MODEL), D_FF),
    }


def reference(x, mem, positions, g_ffn1, w_ffn1_gate, w_ffn1_up, w_ffn1_down,
              g_mix, w_in, g_q, g_k, sinks, w_dw, b_dw, g_conv_ln, b_conv_ln,
              g_mem, w_mem_kv, g_mq, g_mk, w_out,
              g_ffn2, w_ffn2_gate, w_ffn2_up, w_ffn2_down):
    B, T = x.shape[0], x.shape[1]
    M = mem.shape[1]
    for l in range(DEPTH):
        x = x + 0.5 * swiglu(rms_norm(x, g_ffn1[l]), w_ffn1_gate[l], w_ffn1_up[l], w_ffn1_down[l])

        h = rms_norm(x, g_mix[l])
        proj = h @ w_in[l]
        q, k, v, u, mq = jnp.split(proj, SPLITS, axis=-1)
        q = q.reshape(B, T, N_Q_HEADS, HEAD_DIM)
        k = k.reshape(B, T, N_KV_HEADS, HEAD_DIM)
        v = v.reshape(B, T, N_KV_HEADS, HEAD_DIM)
        q = rope_partial(rms_norm(q, g_q[l]), positions)
        k = rope_partial(rms_norm(k, g_k[l]), positions)
        y_attn = sliding_window_attention(q, k, v, sinks[l])

        y_conv = conformer_conv(u, w_dw[l], b_dw[l], g_conv_ln[l], b_conv_ln[l])

        mkv = rms_norm(mem, g_mem[l]) @ w_mem_kv[l]
        mk, mv = jnp.split(mkv, 2, axis=-1)
        mq = rms_norm(mq.reshape(B, T, N_MEM_HEADS, HEAD_DIM), g_mq[l])
        mk = rms_norm(mk.reshape(B, M, N_MEM_HEADS, HEAD_DIM), g_mk[l])
        mv = mv.reshape(B, M, N_MEM_HEADS, HEAD_DIM)
        y_mem = memory_cross_attention(mq, mk, mv)

        x = x + jnp.concatenate([y_attn, y_conv, y_mem], axis=-1) @ w_out[l]

        x = x + 0.5 * swiglu(rms_norm(x, g_ffn2[l]), w_ffn2_gate[l], w_ffn2_up[l], w_ffn2_down[l])
    return x
```

```python
import math
import os
from contextlib import ExitStack

import numpy as np
import concourse.bass as bass
import concourse.mybir as mybir
from concourse.bass_utils import run_bass_kernel_spmd

F32 = mybir.dt.float32
BF16 = mybir.dt.bfloat16
I32 = mybir.dt.int32
ALU = mybir.AluOpType
AF = mybir.ActivationFunctionType
AX = mybir.AxisListType

T = 2048
D = 1024
DFF = 2816
NFC = DFF // 128
GROUPS = [(0, 4), (4, 4), (8, 4), (12, 4), (16, 3), (19, 3)]
EPS = 1e-6
NT = T // 512
NB = T // 128


class Prog:
    ENGS = ("pe", "act", "dve", "pool", "sp")

    def __init__(self, nc, stack):
        self.nc = nc
        self.sem = {}
        self.cnt = {}
        self.waited = {e: {} for e in self.ENGS}
        self.stream = {e: [] for e in self.ENGS}
        self.lastw = {}
        self.reads = {}
        self.stack = stack
        for e in self.ENGS:
            self.sem[e] = stack.enter_context(nc.semaphore("s_" + e))
            self.cnt[e] = 0

    def _need(self, eng, ev):
        if ev is None:
            return
        s, c = ev
        if s == eng and c > self.cnt[eng]:
            return
        if self.waited[eng].get(s, 0) >= c:
            return
        self.waited[eng][s] = c
        self.stream[eng].append(("wait", s, c))

    def _deps(self, eng, reads, writes):
        for k in reads:
            self._need(eng, self.lastw.get(k))
        for k in writes:
            self._need(eng, self.lastw.get(k))
            for s, c in self.reads.get(k, {}).items():
                self._need(eng, (s, c))

    def _record(self, ev, reads, writes):
        s, c = ev
        for k in reads:
            self.reads.setdefault(k, {})[s] = c
        for k in writes:
            self.lastw[k] = ev
            self.reads[k] = {}

    def op(self, eng, fn, reads=(), writes=(), inc=True):
        bank_r = [k for k in reads if isinstance(k, tuple) and k[0] == "bank"]
        if bank_r:
            reads = [k for k in reads if k not in bank_r]
            writes = list(writes) + bank_r
        self._deps(eng, reads, writes)
        if inc:
            self.cnt[eng] += 1
            ev = (eng, self.cnt[eng])
            self.stream[eng].append(("ins", fn, eng, 1))
        else:
            ev = (eng, self.cnt[eng] + 1)
            self.stream[eng].append(("ins", fn, None, 0))
        self._record(ev, reads, writes)

    def dma(self, q, semname, out, in_, reads=(), writes=(), deps=True, **kw):
        if semname not in self.sem:
            self.sem[semname] = self.stack.enter_context(self.nc.semaphore("d_" + semname))
            self.cnt[semname] = 0
        if deps:
            self._deps(q, reads, writes)
        self.cnt[semname] += 16
        ev = (semname, self.cnt[semname])
        self.stream[q].append(
            ("ins", lambda h: h.dma_start(out=out, in_=in_, **kw), semname, 16))
        self._record(ev, reads, writes)

    def wait_write(self, eng, keys):
        for k in keys:
            self._need(eng, self.lastw.get(k))

    def wait_all(self, eng, keys):
        for k in keys:
            self._need(eng, self.lastw.get(k))
            for s, c in self.reads.get(k, {}).items():
                self._need(eng, (s, c))

    def emit(self):
        nc = self.nc
        with nc.Block() as block:
            def run(e):
                def body(h):
                    for it in self.stream[e]:
                        if it[0] == "wait":
                            h.wait_ge(self.sem[it[1]], it[2])
                        else:
                            _, fn, s, n = it
                            ins = fn(h)
                            if s is not None:
                                ins.then_inc(self.sem[s], n)
                return body
            block.tensor(run("pe"))
            block.scalar(run("act"))
            block.vector(run("dve"))
            block.gpsimd(run("pool"))
            block.sync(run("sp"))


def build_nc(stages=("ffn1", "mix", "ffn2"), debug=False):
    nc = bass.Bass("TRN2", target_bir_lowering=False)

    def din(name, shape, dt=F32):
        return nc.dram_tensor(name, shape, dt, kind="ExternalInput").ap()

    x_d = din("x", [T, D])
    mem_d = din("mem", [256, D])
    pos_d = din("positions", [NB, 128], I32)
    invf_d = din("invfreq", [1, 8])
    g_ffn_d = [din("g_ffn1", [8, 128]), din("g_ffn2", [8, 128])]
    wffn_d = [din("wffn1", [128, NFC * 3072]), din("wffn2", [128, NFC * 3072])]
    wmix0_d = din("wmix0", [128, 12288])
    wmix1_d = din("wmix1", [128, 12288])
    g_mix_d = din("g_mix", [8, 128])
    g_q_d = din("g_q", [1, 64])
    g_k_d = din("g_k", [1, 64])
    sinks_d = din("sinks", [2, 4])
    w_dw_d = din("w_dw", [31, 256])
    b_dw_d = din("b_dw", [2, 128])
    g_cln_d = din("g_conv_ln", [2, 128])
    b_cln_d = din("b_conv_ln", [2, 128])
    g_mem_d = din("g_mem", [8, 128])
    g_mq_d = din("g_mq", [1, 64])
    g_mk_d = din("g_mk", [1, 64])
    out_d = nc.dram_tensor("out", [T, D], F32, kind="ExternalOutput").ap()

    with ExitStack() as st:
        P = Prog(nc, st)

        def sb(name, shape, dt):
            return st.enter_context(nc.sbuf_tensor(name, shape, dt))

        xT = sb("xT", [128, 8, T], F32)
        hT = sb("hT", [128, 8, T], BF16)
        ring = sb("ring", [128, 2, 12288], BF16)
        aT = sb("aT", [128, 2, 4, 512], BF16)
        kT = sb("kT", [128, T], BF16)
        vtok = sb("vtok", [128, NB, 128], BF16)
        mqT = sb("mqT", [128, 2, 2, 512], BF16)
        hglu = sb("hglu", [128, 2, 2, 32 + 512], BF16)
        catT = sb("catT", [128, 8, 512], BF16)
        xstage = sb("xstage", [128, 2, D], F32)
        PT = xstage[:, :, :].bitcast(BF16).rearrange("p b (s t) -> p b s t", s=4)
        mkT = sb("mkT", [128, 2, 256], BF16)
        mvtok = sb("mvtok", [128, 2, 256], BF16)
        ident_f = sb("ident_f", [128, 128], F32)
        ident_b = sb("ident_b", [128, 128], BF16)
        ones_d = sb("ones_d", [128, 128], BF16)
        ones_1 = sb("ones_1", [128, 64], BF16)
        colsT = sb("colsT", [128, 100], F32)
        posf = sb("posf", [128, NB], F32)
        gtmp = sb("gtmp", [128, 4, 64], F32)
        invf = sb("invf", [128, 8], F32)
        cs = sb("cs", [128, 2, NB, 8], F32)
        mask_b = sb("mask_b", [128, 2, 128], BF16)
        esink = sb("esink", [128, 4], F32)
        small = sb("small", [128, 16], F32)
        rstd = sb("rstd", [128, 1, 512], F32)
        sg = sb("sg", [128, 2, 512], F32)
        qk_f = sb("qk_f", [128, 16, 64], F32)
        mask_f = qk_f[:, 4:8, :].rearrange("p (a b) c -> p a (b c)", a=2)
        prow = qk_f[:, 0:2, :].rearrange("p a c -> p (a c)")
        qk_b = sb("qk_b", [128, 7, 128], BF16)
        rtmp = sb("rtmp", [128, 4, 10, 8], F32)
        pos_i = rtmp[0:NB, :, :, :].rearrange("p a b c -> p (a b c)")[:, 0:128].bitcast(I32)
        st16 = sb("st16", [128, 3, 16], F32)
        rs2t = sb("rs2t", [128, 512], F32)
        cv = sg
        cvb = sb("cvb", [128, 2, 512], BF16)

        banks = [st.enter_context(nc.psum_tensor("pb%d" % i, [128, 512], F32)) for i in range(8)]
        bk = lambda i: ("bank", i)

        P.op("pool", lambda h: h.memset(ident_f[:, :], 0.0), writes=["ident_f"])
        P.op("pool", lambda h: h.affine_select(out=ident_f[:, :], in_=ident_f[:, :], compare_op=ALU.not_equal,
                                                fill=1.0, base=0, pattern=[[-1, 128]], channel_multiplier=1),
             reads=["ident_f"], writes=["ident_f"])
        P.op("pool", lambda h: h.tensor_copy(out=ident_b[:, :], in_=ident_f[:, :]), reads=["ident_f"], writes=["ident_b"])
        DBG = os.environ.get("KDBG", "")
        if "noconst" not in DBG:
            P.op("pool", lambda h: h.memset(ones_d[:, :], 1.0 / 1024), writes=["ones_d"])
            P.op("pool", lambda h: h.memset(ones_1[:, :], 1.0), writes=["ones_1"])
            P.op("pool", lambda h: h.memset(small[:, :], 0.0), writes=["small"])
            P.op("pool", lambda h: h.memset(small[:, 0:1], -math.pi), reads=["small"], writes=["small"])
            P.op("pool", lambda h: h.memset(small[:, 1:2], -0.5), reads=["small"], writes=["small"])
            P.op("pool", lambda h: h.memset(small[:, 2:3], 2.0 * math.pi), reads=["small"], writes=["small"])
            P.op("pool", lambda h: h.memset(prow[:, :], 0.0), writes=["prow"])
            SQ = "sp"
            P.op("pool", lambda h: h.memset(hglu[:, 0, :, 0:32], 0.0), writes=[("hglu", 0, 0), ("hglu", 0, 1)])
            if "nomask" not in DBG:
              P.op("pool", lambda h: h.memset(mask_f[:, :, :], 1.0), writes=["mask_f"])
            P.op("pool", lambda h: h.affine_select(out=mask_f[:, 0, :], in_=mask_f[:, 0, :], compare_op=ALU.is_gt,
                                                    fill=0.0, base=0, pattern=[[-1, 128]], channel_multiplier=1),
                 reads=["mask_f"], writes=["mask_f"])
            P.op("pool", lambda h: h.affine_select(out=mask_f[:, 1, :], in_=mask_f[:, 1, :], compare_op=ALU.is_ge,
                                                    fill=0.0, base=0, pattern=[[1, 128]], channel_multiplier=-1),
                 reads=["mask_f"], writes=["mask_f"])
            P.op("pool", lambda h: h.tensor_copy(out=mask_b[:, :, :], in_=mask_f[:, :, :]), reads=["mask_f"], writes=["mask_b"])

            R_GF1, R_GMIX, R_GF2, R_GMEM, R_WDW, R_BDW, R_GCLN, R_BCLN = 0, 8, 16, 24, 32, 94, 96, 98
            small_loads = [(R_GF1, 8, g_ffn_d[0]), (R_GMIX, 8, g_mix_d), (R_GF2, 8, g_ffn_d[1]), (R_GMEM, 8, g_mem_d),
                           (R_WDW, 31, w_dw_d[:, 0:128]), (R_WDW + 31, 31, w_dw_d[:, 128:256]),
                           (R_BDW, 2, b_dw_d), (R_GCLN, 2, g_cln_d), (R_BCLN, 2, b_cln_d)]
            for r0, n, src in small_loads:
                P.dma(SQ, "small", prow[r0:r0 + n, :], src, reads=["prow"], writes=[("prow", r0)])
            P.dma(SQ, "small", pos_i[:, :], pos_d, writes=["pos_i"])
            P.dma(SQ, "small", invf[:, :], invf_d.partition_broadcast(128), writes=["invf"])
            P.dma(SQ, "small", gtmp[:, 0, :], g_q_d.partition_broadcast(128), writes=[("gtmp", len(P.stream[SQ]))])
            P.dma(SQ, "small", gtmp[:, 1, :], g_k_d.partition_broadcast(128), writes=[("gtmp", len(P.stream[SQ]))])
            P.dma(SQ, "small", gtmp[:, 2, :], g_mq_d.partition_broadcast(128), writes=[("gtmp", len(P.stream[SQ]))])
            P.dma(SQ, "small", gtmp[:, 3, :], g_mk_d.partition_broadcast(128), writes=[("gtmp", len(P.stream[SQ]))])
            P.dma(SQ, "small", esink[0:64, :], sinks_d[0:1, :].partition_broadcast(64), writes=[("esink", 0)])
            P.dma(SQ, "small", esink[64:128, :], sinks_d[1:2, :].partition_broadcast(64), writes=[("esink", 1)])

            for k, ev in list(P.lastw.items()):
                if ev[0] == "small":
                    P.lastw[k] = ("small", P.cnt["small"])

        def load_ffn_group(f, gi, after=()):
            f0, G = GROUPS[gi]
            slot = load_ffn_group.n % 2
            load_ffn_group.n += 1
            key = ("ring", slot)
            sem = "ring%d" % slot
            off = f0 * 3072
            P.wait_write("pool", after)
            P.dma("pool", sem, ring[:, slot, 0:G * 3072], wffn_d[f][:, off:off + G * 3072], writes=[key])
            return slot
        load_ffn_group.n = 0

        def load_mixer_weights():
            P.dma("pool", "ring0", ring[:, 0, :], wmix0_d[:, :], writes=[("ring", 0)])

        def load_mixer_weights2():
            P.dma("pool", "ring1", ring[:, 1, :], wmix1_d[:, :], writes=[("ring", 1)])


        if "noconst" not in DBG:
            P.op("pe", lambda h: h.transpose(out=banks[7][:, 0:128], in_=prow[:, :], identity=ident_f[:, :]),
                 reads=[("prow", r0) for r0, _, _ in small_loads] + ["ident_f"], writes=[bk(7)])
            P.op("dve", lambda h: h.tensor_copy(out=colsT[:, :], in_=banks[7][:, 0:100]), reads=[bk(7)], writes=["colsT"])
        col = lambda r: colsT[:, r:r + 1]

        def load_x_tile(b):
            s = b % 2
            P.dma("sp", "xs%d" % s, xstage[:, s, :], x_d[b * 128:(b + 1) * 128, :], writes=[("xstage", s)])

        def transpose_x_tile(b):
            s = b % 2
            for half in range(2):
                bank = banks[(2 * b + half) % 4]
                key = bk((2 * b + half) % 4)
                for c4 in range(4):
                    c = half * 4 + c4
                    P.op("pe", lambda h, c=c, c4=c4, bank=bank: h.transpose(
                        out=bank[:, c4 * 128:(c4 + 1) * 128], in_=xstage[:, s, c * 128:(c + 1) * 128],
                        identity=ident_f[:, :]),
                        reads=[("xstage", s), "ident_f"], writes=[key], inc=(c4 == 3))
                src = bank[:, :].rearrange("p (c t) -> p c t", c=4)
                dst = xT[:, half * 4:half * 4 + 4, b * 128:(b + 1) * 128]
                wk = [("xT", c, b // 4) for c in range(half * 4, half * 4 + 4)]
                if half == 0:
                    P.op("act", lambda h, src=src, dst=dst: h.copy(out=dst, in_=src), reads=[key], writes=wk)
                else:
                    P.op("dve", lambda h, src=src, dst=dst: h.tensor_copy(out=dst, in_=src), reads=[key], writes=wk)

        hflat = hT[:, :, :].rearrange("p c t -> p (c t)")
        hTm = hflat[:, 0:8192].rearrange("p (b c t) -> p b c t", b=2, c=8)
        dstore = hflat[:, 8192:8192 + 62 * 128].rearrange("p (n m) -> p n m", n=62)
        ALL_HT = [("hT", c, t) for c in range(8) for t in range(NT)]
        ALL_HTM = [("hTm", b_, c) for b_ in range(2) for c in range(8)] + ["dstore"]

        def norm_tile(t, grow, nbuf, mbuf=None, extra_w=(), part=None):
            tok = slice(t * 512, (t + 1) * 512)
            xk = [("xT", c, t) for c in range(8)]
            if mbuf is None:
                dst = hT[:, :, tok]
                hkey = lambda c: ("hT", c, t)
                pbank = 6 + (nbuf % 2)
            else:
                dst = hTm[:, mbuf, :, :]
                hkey = lambda c: ("hTm", mbuf, c)
                pbank = 7
            hk = [hkey(c) for c in range(8)]
            if part in (None, "sq"):
                P.op("act", lambda h: h.activation(out=dst, in_=xT[:, :, tok], func=AF.Square),
                     reads=xk, writes=hk + list(extra_w))
            if part == "sq":
                return
            for c in range(8):
                P.op("pe", lambda h, c=c: h.matmul(banks[pbank][:, :], lhsT=ones_d[:, :], rhs=dst[:, c, :],
                                                   start=(c == 0), stop=(c == 7)),
                     reads=[hkey(c), "ones_d"], writes=[bk(pbank)], inc=(c == 7))
            i = 0
            P.op("dve", lambda h: h.tensor_scalar_add(out=rstd[:, i, :], in0=banks[pbank][:, :], scalar1=EPS),
                 reads=[bk(pbank)], writes=[("rstd", i)])
            P.op("act", lambda h: h.activation(out=rstd[:, i, :], in_=rstd[:, i, :], func=AF.Ln),
                 reads=[("rstd", i)], writes=[("rstd", i)])
            P.op("act", lambda h: h.activation(out=rstd[:, i, :], in_=rstd[:, i, :], func=AF.Exp, scale=-0.5),
                 reads=[("rstd", i)], writes=[("rstd", i)])
            for c in range(8):
                P.op("dve", lambda h, c=c: h.scalar_tensor_tensor(
                    out=dst[:, c, :], in0=xT[:, c, tok], scalar=col(grow + c), in1=rstd[:, i, :],
                    op0=ALU.mult, op1=ALU.mult),
                    reads=[("xT", c, t), "colsT", ("rstd", i)], writes=[hkey(c)])

        def ffn(f, after_stage=None):
            stagesl = [(gi, t) for gi in range(len(GROUPS)) for t in range(NT)]
            slots = {}
            slots[0] = 0 if f == 0 else ffn.pre[0]
            slots[1] = 1 if f == 0 else ffn.pre[1]

            def GU(i):
                gi, t = stagesl[i]
                f0, G = GROUPS[gi]
                slot = slots[gi]
                tok = slice(t * 512, (t + 1) * 512)
                ab = i % 2
                gw = G * 128
                wgv = ring[:, slot, 0:8 * gw].rearrange("p (k f) -> p k f", k=8)
                wuv = ring[:, slot, 8 * gw:16 * gw].rearrange("p (k f) -> p k f", k=8)
                for fi in range(G):
                    gb = (GU.n % 2)
                    ub = 2 + (GU.n % 2)
                    GU.n += 1
                    for (bnk, wv) in ((gb, wgv), (ub, wuv)):
                        for kc in range(8):
                            P.op("pe", lambda h, bnk=bnk, wv=wv, kc=kc, fi=fi: h.matmul(
                                banks[bnk][:, :], lhsT=wv[:, kc, fi * 128:(fi + 1) * 128], rhs=hT[:, kc, tok],
                                start=(kc == 0), stop=(kc == 7)),
                                reads=[("ring", slot), ("hT", kc, t)], writes=[bk(bnk)], inc=(kc == 7))
                    sgi = gb
                    P.op("act", lambda h, gb=gb, sgi=sgi: h.activation(out=sg[:, sgi, :], in_=banks[gb][:, :], func=AF.Silu),
                         reads=[bk(gb)], writes=[("sg", sgi)])
                    P.op("dve", lambda h, ub=ub, sgi=sgi, fi=fi: h.tensor_tensor(
                        out=aT[:, ab, fi, :], in0=sg[:, sgi, :], in1=banks[ub][:, :], op=ALU.mult),
                        reads=[("sg", sgi), bk(ub)], writes=[("aT", ab, fi)])
            GU.n = 0

            def Dn(i):
                gi, t = stagesl[i]
                f0, G = GROUPS[gi]
                slot = slots[gi]
                tok = slice(t * 512, (t + 1) * 512)
                ab = i % 2
                wdv = ring[:, slot, 16 * G * 128:16 * G * 128 + G * 1024].rearrange("p (g d) -> p g d", g=G)
                for dc in range(8):
                    bnk = 4 + (Dn.n % 2)
                    Dn.n += 1
                    for fi in range(G):
                        P.op("pe", lambda h, bnk=bnk, fi=fi, dc=dc: h.matmul(
                            banks[bnk][:, :], lhsT=wdv[:, fi, dc * 128:(dc + 1) * 128], rhs=aT[:, ab, fi, :],
                            start=(fi == 0), stop=(fi == G - 1)),
                            reads=[("ring", slot), ("aT", ab, fi)], writes=[bk(bnk)], inc=(fi == G - 1))
                    P.op("dve", lambda h, bnk=bnk, dc=dc: h.scalar_tensor_tensor(
                        out=xT[:, dc, tok], in0=banks[bnk][:, :], scalar=0.5, in1=xT[:, dc, tok],
                        op0=ALU.mult, op1=ALU.add),
                        reads=[bk(bnk), ("xT", dc, t)], writes=[("xT", dc, t)])
                if t == NT - 1:
                    nxt = gi + 2
                    if nxt < len(GROUPS):
                        slots[nxt] = load_ffn_group(f, nxt)
                    elif after_stage is not None:
                        after_stage(gi)
                if gi == len(GROUPS) - 1 and ffn.final is not None:
                    ffn.final(t)
            Dn.n = 0

            for i in range(len(stagesl)):
                GU(i)
                if i >= 1:
                    Dn(i - 1)
            Dn(len(stagesl) - 1)
        ffn.pre = {}
        ffn.final = None

        def mixer():
            win = ring[:, 0, :].rearrange("p (k f) -> p k f", k=8)
            wout = ring[:, 1, 0:8192].rearrange("p (c d) -> p c d", c=8)
            wmkv = ring[:, 1, 8192:12288].rearrange("p (k f) -> p k f", k=8)
            qkf2 = qk_f[:, :, :].rearrange("p a c -> p (a c)")
            qkb2 = qk_b[:, :, :].rearrange("p a c -> p (a c)")
            memn = cvb[:, :, :].rearrange("p a (c t) -> p (a c) t", c=4)
            memT = catT[:, :, 0:256]
            b2v = banks[2][:, :].bitcast(BF16)
            b6v = banks[6][:, :].bitcast(BF16)
            rs = rstd[:, 0, :]
            rs2 = rs2t[:, :]
            RK = ("rstd", 0)
            negm_swa = small[:, 3:4]
            negm_mem = small[:, 4:5]

            P.op("dve", lambda h: h.reduce_max(out=st16[:, 0, 0:4], in_=gtmp[:, :, :], axis=AX.X, apply_absolute_value=True),
                 reads=[k for k in P.lastw if isinstance(k, tuple) and k[0] == "gtmp"], writes=["st16a"])
            P.op("dve", lambda h: h.tensor_tensor(out=small[:, 3:4], in0=st16[:, 0, 0:1], in1=st16[:, 0, 1:2], op=ALU.mult),
                 reads=["st16a", "small"], writes=["small"])
            P.op("dve", lambda h: h.tensor_tensor(out=small[:, 4:5], in0=st16[:, 0, 2:3], in1=st16[:, 0, 3:4], op=ALU.mult),
                 reads=["st16a", "small"], writes=["small"])
            P.op("dve", lambda h: h.tensor_scalar_mul(out=small[:, 3:5], in0=small[:, 3:5], scalar1=-8.0),
                 reads=["small"], writes=["small"])
            P.op("act", lambda h: h.activation(out=esink[:, :], in_=esink[:, :], func=AF.Exp, bias=negm_swa, scale=1.0),
                 reads=[("esink", 0), ("esink", 1), "small"], writes=["esinkx"])

            P.op("dve", lambda h: h.memset(st16[:, 1, :], 0.0), writes=["st16b"])
            for mt in range(2):
                P.op("act", lambda h, mt=mt: h.activation(out=qkf2, in_=xstage[:, mt, :], func=AF.Square,
                                                           accum_out=st16[:, 1, mt:mt + 1]),
                     reads=[("xstage", mt), "st16b"],
                     writes=["qk_f", "st16b"] + (["mask_f", "prow", "prow2", "ang"] + [("prow", r0) for r0, _, _ in small_loads]
                                                 if mt == 0 else []))
                P.op("dve", lambda h, mt=mt: h.tensor_scalar(out=st16[:, 1, mt:mt + 1], in0=st16[:, 1, mt:mt + 1],
                                                              scalar1=1.0 / 1024, scalar2=EPS, op0=ALU.mult, op1=ALU.add),
                     reads=["st16b"], writes=["st16b"])
                P.op("act", lambda h, mt=mt: h.activation(out=st16[:, 1, mt:mt + 1], in_=st16[:, 1, mt:mt + 1], func=AF.Ln),
                     reads=["st16b"], writes=["st16b"])
                P.op("act", lambda h, mt=mt: h.activation(out=st16[:, 1, mt:mt + 1], in_=st16[:, 1, mt:mt + 1], func=AF.Exp, scale=-0.5),
                     reads=["st16b"], writes=["st16b"])
                P.op("act", lambda h, mt=mt: h.activation(out=cvb[:, :, :].rearrange("p a t -> p (a t)"), in_=xstage[:, mt, :],
                                                           func=AF.Copy, scale=st16[:, 1, mt:mt + 1]),
                     reads=[("xstage", mt), "st16b"], writes=["cvb"])
                for c in range(8):
                    P.op("pe", lambda h, c=c: h.transpose(out=b6v[:, c * 128:(c + 1) * 128], in_=memn[:, c, :],
                                                          identity=ident_b[:, :]),
                         reads=["cvb", "ident_b"], writes=[bk(6)], inc=(c == 7))
                P.op("dve", lambda h, mt=mt: h.tensor_tensor(
                    out=memT[:, :, mt * 128:(mt + 1) * 128], in0=b6v.rearrange("p (c t) -> p c t", c=8),
                    in1=colsT[:, R_GMEM:R_GMEM + 8].unsqueeze(2).to_broadcast([128, 8, 128]), op=ALU.mult),
                    reads=[bk(6), "colsT"], writes=["catT_mem"])
            def mem_prep2():
                for mt in range(2):
                    for kc in range(8):
                        P.op("pe", lambda h, kc=kc, mt=mt: h.matmul(banks[7][:, :], lhsT=memT[:, kc, mt * 128:(mt + 1) * 128],
                                                                    rhs=wmkv[:, kc, :], start=(kc == 0), stop=(kc == 7)),
                             reads=["catT_mem", ("ring", 1)], writes=[bk(7)], inc=(kc == 7))
                    P.op("act", lambda h: h.activation(out=qkf2[:, 0:256], in_=banks[7][:, 0:256], func=AF.Square),
                         reads=[bk(7)], writes=["qk_f"])
                    P.op("dve", lambda h: h.reduce_sum(out=st16[:, 2, 0:4], in_=qk_f[:, 0:4, :], axis=AX.X),
                         reads=["qk_f"], writes=["st16c"])
                    P.op("dve", lambda h: h.tensor_scalar(out=st16[:, 2, 0:4], in0=st16[:, 2, 0:4], scalar1=1.0 / 64, scalar2=EPS,
                                                           op0=ALU.mult, op1=ALU.add), reads=["st16c"], writes=["st16c"])
                    P.op("act", lambda h: h.activation(out=st16[:, 2, 0:4], in_=st16[:, 2, 0:4], func=AF.Ln),
                         reads=["st16c"], writes=["st16c"])
                    P.op("act", lambda h: h.activation(out=st16[:, 2, 0:4], in_=st16[:, 2, 0:4], func=AF.Exp, scale=-0.5),
                         reads=["st16c"], writes=["st16c"])
                    P.op("dve", lambda h: h.tensor_tensor(out=qk_f[:, 0:4, :], in0=banks[7][:, 0:256].rearrange("p (a c) -> p a c", a=4),
                                                          in1=gtmp[:, 3:4, :].to_broadcast([128, 4, 64]), op=ALU.mult),
                         reads=[bk(7), "qk_f"], writes=["qk_f"])
                    P.op("dve", lambda h: h.tensor_tensor(out=qk_b[:, 0:2, :].rearrange("p a (b c) -> p (a b) c", b=2),
                                                          in0=qk_f[:, 0:4, :],
                                                          in1=st16[:, 2, 0:4].unsqueeze(2).to_broadcast([128, 4, 64]), op=ALU.mult),
                         reads=["qk_f", "st16c"], writes=["qk_b"])
                    P.op("act", lambda h, mt=mt: h.copy(out=mvtok[:, mt, :], in_=banks[7][:, 256:512]),
                         reads=[bk(7)], writes=["mvtok"])
                    for j in range(2):
                        P.op("pe", lambda h, j=j: h.transpose(out=b6v[:, j * 128:(j + 1) * 128], in_=qk_b[:, j, :],
                                                              identity=ident_b[:, :]),
                             reads=["qk_b", "ident_b"], writes=[bk(6)], inc=(j == 1))
                    P.op("dve", lambda h, mt=mt: h.tensor_copy(out=mkT[:, :, mt * 128:(mt + 1) * 128],
                                                                in_=b6v[:, 0:256].rearrange("p (c t) -> p c t", c=2)),
                         reads=[bk(6)], writes=["mkT"])


            stt_cnt = [0]

            def st_bank():
                stt_cnt[0] += 1
                return 5 + (stt_cnt[0] % 2)

            def proj_mm(b):
                Tt = b // 4
                mb = Tt % 2
                l128 = slice((b % 4) * 128, (b % 4 + 1) * 128)
                for kc in range(8):
                    P.op("pe", lambda h, kc=kc: h.matmul(banks[0][:, :], lhsT=hTm[:, mb, kc, l128], rhs=win[:, kc, 0:512],
                                                         start=(kc == 0), stop=(kc == 7)),
                         reads=[("hTm", mb, kc), ("ring", 0)], writes=[bk(0)], inc=(kc == 7))
                for kc in range(8):
                    P.op("pe", lambda h, kc=kc: h.matmul(banks[1][:, :], lhsT=hTm[:, mb, kc, l128], rhs=win[:, kc, 512:1024],
                                                         start=(kc == 0), stop=(kc == 7)),
                         reads=[("hTm", mb, kc), ("ring", 0)], writes=[bk(1)], inc=(kc == 7))

            sqb = xstage[:, 1, :]

            def normrope_a(b):
                P.op("act", lambda h: h.activation(out=sqb[:, 0:512], in_=banks[0][:, :], func=AF.Square),
                     reads=[bk(0)], writes=[("xstage", 1)])
                P.op("act", lambda h: h.activation(out=sqb[:, 512:1024], in_=banks[1][:, :], func=AF.Square),
                     reads=[bk(1)], writes=[("xstage", 1)])

            def normrope_b(b):
                P.op("dve", lambda h: h.reduce_sum(out=st16[:, 0, :], in_=sqb.rearrange("p (a d) -> p a d", d=64), axis=AX.X),
                     reads=[("xstage", 1)], writes=["st16a"])
                P.op("dve", lambda h: h.tensor_scalar(out=st16[:, 0, :], in0=st16[:, 0, :], scalar1=1.0 / 64, scalar2=EPS,
                                                       op0=ALU.mult, op1=ALU.add), reads=["st16a"], writes=["st16a"])
                P.op("act", lambda h: h.activation(out=st16[:, 0, :], in_=st16[:, 0, :], func=AF.Ln),
                     reads=["st16a"], writes=["st16a"])
                P.op("act", lambda h: h.activation(out=st16[:, 0, :], in_=st16[:, 0, :], func=AF.Exp, scale=-0.5),
                     reads=["st16a"], writes=["st16a"])
                P.op("dve", lambda h: h.tensor_tensor(out=qk_f[:, 0:8, :], in0=banks[0][:, :].rearrange("p (a c) -> p a c", a=8),
                                                      in1=gtmp[:, 0:1, :].to_broadcast([128, 8, 64]), op=ALU.mult),
                     reads=[bk(0), "qk_f"], writes=["qk_f"])
                P.op("dve", lambda h: h.tensor_tensor(out=qk_f[:, 8:10, :], in0=banks[1][:, 0:128].rearrange("p (a c) -> p a c", a=2),
                                                      in1=gtmp[:, 1:2, :].to_broadcast([128, 2, 64]), op=ALU.mult),
                     reads=[bk(1), "qk_f"], writes=["qk_f"])
                P.op("dve", lambda h: h.tensor_tensor(out=qk_f[:, 12:16, :], in0=banks[1][:, 256:512].rearrange("p (a c) -> p a c", a=4),
                                                      in1=gtmp[:, 2:3, :].to_broadcast([128, 4, 64]), op=ALU.mult),
                     reads=[bk(1), "qk_f"], writes=["qk_f"])
                P.op("act", lambda h: h.copy(out=vtok[:, b, :], in_=banks[1][:, 128:256]),
                     reads=[bk(1)], writes=[("vtok", b)])
                x1 = qk_f[:, 0:10, 0:8]
                x2 = qk_f[:, 0:10, 8:16]
                cosb = cs[:, 1, b, :].unsqueeze(1).to_broadcast([128, 10, 8])
                sinb = cs[:, 0, b, :].unsqueeze(1).to_broadcast([128, 10, 8])
                for i_, (xa, tb) in enumerate(((x1, cosb), (x2, sinb), (x2, cosb), (x1, sinb))):
                    P.op("dve", lambda h, i_=i_, xa=xa, tb=tb: h.tensor_tensor(out=rtmp[:, i_, :, :], in0=xa, in1=tb, op=ALU.mult),
                         reads=["qk_f", "cs"], writes=["rtmp", "pos_i"])
                P.op("dve", lambda h: h.tensor_tensor(out=x1, in0=rtmp[:, 0, :, :], in1=rtmp[:, 1, :, :], op=ALU.subtract),
                     reads=["rtmp", "qk_f"], writes=["qk_f"])
                P.op("dve", lambda h: h.tensor_tensor(out=x2, in0=rtmp[:, 2, :, :], in1=rtmp[:, 3, :, :], op=ALU.add),
                     reads=["rtmp", "qk_f"], writes=["qk_f"])
                P.op("dve", lambda h: h.tensor_tensor(
                    out=qk_b[:, 0:4, :].rearrange("p c (g d) -> p g c d", g=2),
                    in0=qk_f[:, 0:8, :].rearrange("p (g c) d -> p g c d", g=2),
                    in1=st16[:, 0, 0:8].rearrange("p (g c) -> p g c", g=2).unsqueeze(3).to_broadcast([128, 2, 4, 64]),
                    op=ALU.mult), reads=["qk_f", "st16a"], writes=["qk_b"])
                P.op("dve", lambda h: h.tensor_tensor(
                    out=qk_b[:, 4, :].rearrange("p (a d) -> p a d", a=2), in0=qk_f[:, 8:10, :],
                    in1=st16[:, 0, 8:10].unsqueeze(2).to_broadcast([128, 2, 64]), op=ALU.mult),
                    reads=["qk_f", "st16a"], writes=["qk_b"])
                P.op("dve", lambda h: h.tensor_tensor(
                    out=qk_b[:, 5:7, :].rearrange("p a (b d) -> p (a b) d", b=2), in0=qk_f[:, 12:16, :],
                    in1=st16[:, 0, 12:16].unsqueeze(2).to_broadcast([128, 4, 64]), op=ALU.mult),
                    reads=["qk_f", "st16a"], writes=["qk_b"])

            def tr_evac(b):
                Tt, s_ = b // 4, b % 4
                qb = Tt % 2
                t128 = slice(b * 128, (b + 1) * 128)
                c128 = slice(s_ * 128, (s_ + 1) * 128)
                for j in range(7):
                    P.op("pe", lambda h, j=j: h.transpose(out=b2v[:, j * 128:(j + 1) * 128], in_=qk_b[:, j, :],
                                                          identity=ident_b[:, :]),
                         reads=["qk_b", "ident_b"], writes=[bk(2)], inc=(j == 6))
                P.op("act", lambda h: h.copy(out=aT[:, qb, :, c128], in_=b2v[:, 0:512].rearrange("p (c t) -> p c t", c=4)),
                     reads=[bk(2)], writes=[("aT", qb, c) for c in range(4)])
                P.op("dve", lambda h: h.tensor_copy(out=kT[:, t128], in_=b2v[:, 512:640]),
                     reads=[bk(2)], writes=[("kT", b)])
                P.op("dve", lambda h: h.tensor_copy(out=mqT[:, qb, :, c128],
                                                    in_=b2v[:, 640:896].rearrange("p (c t) -> p c t", c=2)),
                     reads=[bk(2)], writes=[("mqT", qb, 0), ("mqT", qb, 1)])

            def glu(Tt):
                hb = Tt % 2
                mb = Tt % 2
                for j in range(2):
                    for kc in range(8):
                        P.op("pe", lambda h, j=j, kc=kc: h.matmul(banks[7][:, :], lhsT=win[:, kc, 1280 + j * 128:1280 + (j + 1) * 128],
                                                                  rhs=hTm[:, mb, kc, :], start=(kc == 0), stop=(kc == 7)),
                             reads=[("hTm", mb, kc), ("ring", 0)], writes=[bk(7)], inc=(kc == 7))
                    P.op("act", lambda h, j=j: h.activation(out=sg[:, j, :], in_=banks[7][:, :], func=AF.Sigmoid),
                         reads=[bk(7)], writes=[("sg", j)])
                    for kc in range(8):
                        P.op("pe", lambda h, j=j, kc=kc: h.matmul(banks[2][:, :], lhsT=win[:, kc, 1024 + j * 128:1024 + (j + 1) * 128],
                                                                  rhs=hTm[:, mb, kc, :], start=(kc == 0), stop=(kc == 7)),
                             reads=[("hTm", mb, kc), ("ring", 0)], writes=[bk(2)], inc=(kc == 7))
                    P.op("dve", lambda h, j=j: h.tensor_tensor(out=hglu[:, hb, j, 32:544], in0=sg[:, j, :], in1=banks[2][:, :], op=ALU.mult),
                         reads=[("sg", j), bk(2)], writes=[("hglu", hb, j)])

            def attn1(b):
                Tt, s_ = b // 4, b % 4
                qb = Tt % 2
                pb = 0
                c128 = slice(s_ * 128, (s_ + 1) * 128)
                kts = [1] if b == 0 else [0, 1]
                for g in range(2):
                    gp = slice(g * 64, (g + 1) * 64)
                    for kt in kts:
                        kb = b - 1 + kt
                        sbk = st_bank()
                        P.op("pe", lambda h, sbk=sbk, gp=gp, kb=kb: h.matmul(
                            banks[sbk][:, :], lhsT=kT[gp, kb * 128:(kb + 1) * 128], rhs=aT[gp, qb, :, c128],
                            start=True, stop=True),
                            reads=[("kT", kb)] + [("aT", qb, c) for c in range(4)], writes=[bk(sbk)])
                        idx = kt * 2 + g
                        P.op("act", lambda h, sbk=sbk, idx=idx: h.activation(
                            out=PT[:, pb, idx, :], in_=banks[sbk][:, :], func=AF.Exp, bias=negm_swa, scale=0.125),
                            reads=[bk(sbk), "small"], writes=[("PT", pb, idx)])
                        P.op("dve", lambda h, idx=idx, kt=kt: h.tensor_tensor(
                            out=PT[:, pb, idx, :].rearrange("p (c t) -> p c t", c=4),
                            in0=PT[:, pb, idx, :].rearrange("p (c t) -> p c t", c=4),
                            in1=mask_b[:, kt, :].unsqueeze(1).to_broadcast([128, 4, 128]), op=ALU.mult),
                            reads=[("PT", pb, idx), "mask_b"], writes=[("PT", pb, idx)])
                for (bn, isden) in ((3, False), (4, True)):
                    for g in range(2):
                        gp = slice(g * 64, (g + 1) * 64)
                        for kt in kts:
                            kb = b - 1 + kt
                            idx = kt * 2 + g
                            lhs = ones_1[:, 0:64] if isden else vtok[:, kb, gp]
                            P.op("pe", lambda h, bn=bn, gp=gp, lhs=lhs, idx=idx, kt=kt: h.matmul(
                                banks[bn][gp, :], lhsT=lhs, rhs=PT[:, pb, idx, :], start=(kt == kts[0]), stop=(kt == 1)),
                                reads=[("vtok", kb), "ones_1", ("PT", pb, idx)], writes=[bk(bn)],
                                inc=(g == 1 and kt == 1))

            def attn2(b):
                Tt, s_ = b // 4, b % 4
                c128 = slice(s_ * 128, (s_ + 1) * 128)
                P.op("dve", lambda h: h.tensor_tensor(out=rs2.rearrange("p (c t) -> p c t", c=4),
                                                      in0=banks[4][:, :].rearrange("p (c t) -> p c t", c=4),
                                                      in1=esink[:, :].unsqueeze(2).to_broadcast([128, 4, 128]), op=ALU.add),
                     reads=[bk(4), "esinkx"], writes=["rs2"])
                P.op("act", lambda h: h.activation(out=rs2, in_=rs2, func=AF.Ln), reads=["rs2"], writes=["rs2"])
                P.op("act", lambda h: h.activation(out=rs2, in_=rs2, func=AF.Exp, scale=-1.0), reads=["rs2"], writes=["rs2"])
                P.op("dve", lambda h: h.tensor_tensor(out=catT[:, 0:4, c128],
                                                      in0=banks[3][:, :].rearrange("p (c t) -> p c t", c=4),
                                                      in1=rs2.rearrange("p (c t) -> p c t", c=4), op=ALU.mult),
                     reads=[bk(3), "rs2"], writes=[("catT", c) for c in range(4)] + (["catT_mem"] if Tt == 0 else []))

            def conv_a(Tt):
                hb = Tt % 2
                if Tt >= 1:
                    P.op("pool", lambda h: h.tensor_copy(out=hglu[:, hb, :, 2:32], in_=hglu[:, 1 - hb, :, 514:544]),
                         reads=[("hglu", 1 - hb, 0), ("hglu", 1 - hb, 1)], writes=[("hglu", hb, 0), ("hglu", hb, 1)])
                for j in range(2):
                    bn = 7
                    for tap in range(31):
                        P.op("pe", lambda h, j=j, tap=tap, bn=bn: h.matmul(
                            banks[bn][:, :], lhsT=dstore[:, j * 31 + tap, :], rhs=hglu[:, hb, j, 2 + tap:2 + tap + 512],
                            start=(tap == 0), stop=(tap == 30)),
                            reads=["dstore", ("hglu", hb, j)], writes=[bk(bn)], inc=(tap == 30))
                    P.op("act", lambda h, j=j, bn=bn: h.activation(out=cv[:, j, :], in_=banks[bn][:, :], func=AF.Identity,
                                                                    bias=col(R_BDW + j), scale=1.0),
                         reads=[bk(bn), "colsT"], writes=[("sg", j)])
                P.op("pool", lambda h: h.tensor_copy(out=cvb[:, :, :], in_=cv[:, :, :]), reads=[("sg", 0), ("sg", 1)], writes=["cvb"])

            def conv_b(Tt):
                for j in range(2):
                    P.op("pe", lambda h, j=j: h.matmul(banks[7][:, :], lhsT=ones_d[:, :], rhs=cvb[:, j, :], start=(j == 0), stop=(j == 1)),
                         reads=["cvb", "ones_d"], writes=[bk(7)], inc=(j == 1))
                for j in range(2):
                    P.op("dve", lambda h, j=j: h.scalar_tensor_tensor(out=cv[:, j, :], in0=banks[7][:, :], scalar=-4.0, in1=cv[:, j, :],
                                                                      op0=ALU.mult, op1=ALU.add),
                         reads=[("sg", j), bk(7)], writes=[("sg", j)])
                P.op("act", lambda h: h.activation(out=cvb[:, :, :], in_=cv[:, :, :], func=AF.Square),
                     reads=[("sg", 0), ("sg", 1)], writes=["cvb"])

            def conv_c(Tt):
                for j in range(2):
                    P.op("pe", lambda h, j=j: h.matmul(banks[7][:, :], lhsT=ones_d[:, :], rhs=cvb[:, j, :], start=(j == 0), stop=(j == 1)),
                         reads=["cvb", "ones_d"], writes=[bk(7)], inc=(j == 1))
                P.op("dve", lambda h: h.tensor_scalar(out=rs, in0=banks[7][:, :], scalar1=4.0, scalar2=EPS, op0=ALU.mult, op1=ALU.add),
                     reads=[bk(7)], writes=[RK])
                P.op("act", lambda h: h.activation(out=rs, in_=rs, func=AF.Ln), reads=[RK], writes=[RK])
                P.op("act", lambda h: h.activation(out=rs, in_=rs, func=AF.Exp, scale=-0.5), reads=[RK], writes=[RK])
                for j in range(2):
                    P.op("dve", lambda h, j=j: h.tensor_tensor(out=cv[:, j, :], in0=cv[:, j, :], in1=rs, op=ALU.mult),
                         reads=[("sg", j), RK], writes=[("sg", j)])
                    P.op("act", lambda h, j=j: h.activation(out=catT[:, 4 + j, :], in_=cv[:, j, :], func=AF.Silu,
                                                             bias=col(R_BCLN + j), scale=col(R_GCLN + j)),
                         reads=[("sg", j), "colsT"], writes=[("catT", 4 + j)] + (["catT_mem"] if Tt == 0 else []))

            def memattn(Tt, j):
                mqb = Tt % 2
                if True:
                    pb = 0
                    for half in range(2):
                        hp = slice(half * 64, (half + 1) * 64)
                        for mt in range(2):
                            sbk = st_bank()
                            idx = half * 2 + mt
                            P.op("pe", lambda h, sbk=sbk, hp=hp, mt=mt: h.matmul(
                                banks[sbk][:, :], lhsT=mkT[hp, j, mt * 128:(mt + 1) * 128], rhs=mqT[hp, mqb, j, :],
                                start=True, stop=True),
                                reads=["mkT", ("mqT", mqb, j)], writes=[bk(sbk)])
                            P.op("act", lambda h, sbk=sbk, idx=idx: h.activation(
                                out=PT[:, pb, idx, :], in_=banks[sbk][:, :], func=AF.Exp, bias=negm_mem, scale=0.125),
                                reads=[bk(sbk), "small"], writes=[("PT", pb, idx)])
                    for (bn, isden) in ((3, False), (4, True)):
                        for half in range(2):
                            hp = slice(half * 64, (half + 1) * 64)
                            hh = 2 * j + half
                            for mt in range(2):
                                idx = half * 2 + mt
                                lhs = ones_1[:, 0:64] if isden else mvtok[:, mt, hh * 64:(hh + 1) * 64]
                                P.op("pe", lambda h, bn=bn, hp=hp, lhs=lhs, idx=idx, mt=mt: h.matmul(
                                    banks[bn][hp, :], lhsT=lhs, rhs=PT[:, pb, idx, :], start=(mt == 0), stop=(mt == 1)),
                                    reads=["mvtok", "ones_1", ("PT", pb, idx)], writes=[bk(bn)],
                                    inc=(half == 1 and mt == 1))
                    P.op("act", lambda h: h.activation(out=rs2, in_=banks[4][:, :], func=AF.Ln), reads=[bk(4)], writes=["rs2"])
                    P.op("act", lambda h: h.activation(out=rs2, in_=rs2, func=AF.Exp, scale=-1.0), reads=["rs2"], writes=["rs2"])
                    P.op("dve", lambda h: h.tensor_tensor(out=catT[:, 6 + j, :], in0=banks[3][:, :], in1=rs2, op=ALU.mult),
                         reads=[bk(3), "rs2"], writes=[("catT", 6 + j)] + (["catT_mem"] if Tt == 0 else []))

            def oproj(Tt, c0, c1):
                tok = slice(Tt * 512, (Tt + 1) * 512)
                for dc in range(8):
                    bn = 7 if dc % 2 == 0 else 2
                    for c in range(c0, c1):
                        P.op("pe", lambda h, bn=bn, c=c, dc=dc: h.matmul(banks[bn][:, :], lhsT=wout[:, c, dc * 128:(dc + 1) * 128],
                                                                         rhs=catT[:, c, :], start=(c == c0), stop=(c == c1 - 1)),
                             reads=[("ring", 1), ("catT", c)], writes=[bk(bn)], inc=(c == c1 - 1))
                    P.op("dve", lambda h, bn=bn, dc=dc: h.tensor_tensor(out=xT[:, dc, tok], in0=xT[:, dc, tok], in1=banks[bn][:, :], op=ALU.add),
                         reads=[bk(bn), ("xT", dc, Tt)], writes=[("xT", dc, Tt)])

            for n in range(62):
                P.op("pool", lambda h, n=n: h.tensor_tensor(out=dstore[:, n, :], in0=ident_b[:, :],
                                                            in1=colsT[:, R_WDW + n:R_WDW + n + 1].to_broadcast([128, 128]), op=ALU.mult),
                     reads=["ident_b", "colsT"], writes=["dstore"] + (ALL_HT if n == 0 else []))
            norm_tile(0, R_GMIX, 0, mbuf=0, extra_w=ALL_HT)
            norm_tile(1, R_GMIX, 1, mbuf=1)
            proj_mm(0)
            normrope_a(0)
            for b in range(NB):
                pt, ph = (b // 4 - 1, b % 4) if b >= 4 else (None, None)
                if b >= 1:
                    attn1(b - 1)
                if pt is not None and ph == 0:
                    conv_b(pt)
                normrope_b(b)
                if b + 1 < NB:
                    proj_mm(b + 1)
                    normrope_a(b + 1)
                if pt is not None and ph == 0:
                    conv_c(pt)
                if b >= 1:
                    attn2(b - 1)
                if pt is not None:
                    if ph == 0:
                        oproj(pt, 0, 4)
                    elif ph == 1:
                        memattn(pt, 0)
                    elif ph == 2:
                        memattn(pt, 1)
                    elif ph == 3:
                        oproj(pt, 4, 8)
                tr_evac(b)
                if b == 0:
                    mem_prep2()
                if b % 4 == 3:
                    Tt = b // 4
                    glu(Tt)
                    if Tt + 2 < NT:
                        norm_tile(Tt + 2, R_GMIX, 0, mbuf=Tt % 2, part="sq")
                    if b == NB - 1 and "ffn2" in stages:
                        ffn.pre[0] = load_ffn_group(1, 0)
                    conv_a(Tt)
                    if Tt + 2 < NT:
                        norm_tile(Tt + 2, R_GMIX, 0, mbuf=Tt % 2, part="rest")
            attn1(NB - 1)
            conv_b(NT - 1)
            conv_c(NT - 1)
            attn2(NB - 1)
            oproj(NT - 1, 0, 4)
            memattn(NT - 1, 0)
            memattn(NT - 1, 1)
            oproj(NT - 1, 4, 8)
            if "ffn2" in stages:
                ffn.pre[1] = load_ffn_group(1, 1)
        mixer.dn = 0

        def rope_prep():
            prow2 = qk_f[:, 8:10, :].rearrange("p a c -> p (a c)")
            angv = qk_f[:, 10:14, :].rearrange("p a c -> p (a c)")
            ang4 = angv.rearrange("p (s b i) -> p s b i", s=2, b=NB)
            P.op("dve", lambda h: h.tensor_copy(out=prow2[0:NB, :], in_=pos_i[:, :]), reads=["pos_i"], writes=["prow2"])
            P.op("pe", lambda h: h.transpose(out=banks[7][:, 0:NB], in_=prow2[0:NB, :], identity=ident_f[0:NB, 0:NB]),
                 reads=["prow2", "ident_f"], writes=[bk(7)])
            P.op("dve", lambda h: h.tensor_copy(out=posf[:, :], in_=banks[7][:, 0:NB]), reads=[bk(7)], writes=["posf"])
            P.op("dve", lambda h: h.tensor_tensor(out=ang4[:, 0, :, :], in0=posf[:, :].unsqueeze(2).to_broadcast([128, NB, 8]),
                                                  in1=invf[:, :].unsqueeze(1).to_broadcast([128, NB, 8]), op=ALU.mult),
                 reads=["posf", "invf"], writes=["ang"])
            P.op("dve", lambda h: h.tensor_scalar_add(out=ang4[:, 1, :, :], in0=ang4[:, 0, :, :], scalar1=0.5 * math.pi),
                 reads=["ang"], writes=["ang"])
            C1 = 6.28125
            C2 = 2.0 * math.pi - C1
            tmpf = sg[:, 0, 0:256]
            tmpi = sg[:, 1, 0:256].bitcast(I32)
            P.op("dve", lambda h: h.tensor_scalar_mul(out=tmpf, in0=angv, scalar1=1.0 / (2.0 * math.pi)),
                 reads=["ang"], writes=[("sg", 0)])
            P.op("dve", lambda h: h.tensor_copy(out=tmpi, in_=tmpf), reads=[("sg", 0)], writes=[("sg", 1)])
            P.op("dve", lambda h: h.tensor_copy(out=tmpf, in_=tmpi), reads=[("sg", 1)], writes=[("sg", 0)])
            P.op("dve", lambda h: h.scalar_tensor_tensor(out=angv, in0=tmpf, scalar=-C1, in1=angv, op0=ALU.mult, op1=ALU.add),
                 reads=[("sg", 0), "ang"], writes=["ang"])
            P.op("dve", lambda h: h.scalar_tensor_tensor(out=angv, in0=tmpf, scalar=-C2, in1=angv, op0=ALU.mult, op1=ALU.add),
                 reads=[("sg", 0), "ang"], writes=["ang"])
            P.op("dve", lambda h: h.tensor_scalar(out=tmpf, in0=angv, scalar1=math.pi, scalar2=-2.0 * math.pi,
                                                   op0=ALU.is_gt, op1=ALU.mult), reads=["ang"], writes=[("sg", 0)])
            P.op("dve", lambda h: h.tensor_tensor(out=angv, in0=angv, in1=tmpf, op=ALU.add), reads=[("sg", 0), "ang"], writes=["ang"])
            P.op("act", lambda h: h.activation(out=cs[:, :, :, :].rearrange("p s b i -> p (s b i)"), in_=angv, func=AF.Sin),
                 reads=["ang", "small"], writes=["cs"])

        def store_tile_512(t):
            for bb in range(4):
                b = t * 4 + bb
                s = b % 2
                for half in range(2):
                    bank = banks[half]
                    for c4 in range(4):
                        c = half * 4 + c4
                        P.op("pe", lambda h, c=c, c4=c4, bank=bank, b=b: h.transpose(
                            out=bank[:, c4 * 128:(c4 + 1) * 128], in_=xT[:, c, b * 128:(b + 1) * 128],
                            identity=ident_f[:, :]),
                            reads=[("xT", c, t), "ident_f"], writes=[bk(half)], inc=(c4 == 3))
                    dst = xstage[:, s, half * 512:(half + 1) * 512]
                    if half == 0:
                        P.op("act", lambda h, dst=dst, bank=bank: h.copy(out=dst, in_=bank[:, :]),
                             reads=[bk(half)], writes=[("xstage", s)])
                    else:
                        P.op("dve", lambda h, dst=dst, bank=bank: h.tensor_copy(out=dst, in_=bank[:, :]),
                             reads=[bk(half)], writes=[("xstage", s)])
                P.dma("sp", "os%d" % s, out_d[b * 128:(b + 1) * 128, :], xstage[:, s, :],
                      reads=[("xstage", s)], writes=[("out", b)])

        if "touch" in DBG:
            for i, d in enumerate([mem_d, wffn_d[0], wffn_d[1], wmix0_d, wmix1_d,
                      g_ffn_d[0], g_ffn_d[1], g_mix_d, g_q_d, g_k_d, sinks_d, w_dw_d, b_dw_d, g_cln_d, b_cln_d,
                      g_mem_d, g_mq_d, g_mk_d, invf_d]):
                P.dma("sp", "touch", st16[0:1, 0, 0:4], d[0:1, 0:4], writes=[("touch", i)])
            P.dma("sp", "touch", pos_i[0:1, 0:4], pos_d[0:1, 0:4], writes=[("touch", 99)])
        load_x_tile(0)
        load_x_tile(1)
        for b in range(NB):
            transpose_x_tile(b)
            if b + 2 < NB:
                load_x_tile(b + 2)
            if "ffn1" in stages and b == 1:
                load_ffn_group(0, 0, after=[("xstage", 0), ("xstage", 1)])
            if "ffn1" in stages and b == NB - 3:
                load_ffn_group(0, 1, after=[("xstage", 0), ("xstage", 1)])
            if "ffn1" in stages and b % 4 == 3:
                norm_tile(b // 4, R_GF1, b // 4)

        if "mix" in stages:
            rope_prep()
            for mt in range(2):
                P.dma("sp", "mem%d" % mt, xstage[:, mt, :], mem_d[mt * 128:(mt + 1) * 128, :], writes=[("xstage", mt)])

        if "ffn1" in stages:
            def after1(gi):
                if "mix" in stages:
                    if gi == len(GROUPS) - 2:
                        load_mixer_weights()
                    else:
                        load_mixer_weights2()
                elif "ffn2" in stages:
                    ffn.pre[gi - (len(GROUPS) - 2)] = load_ffn_group(1, gi - (len(GROUPS) - 2))
            if "ffn2" not in stages and "mix" not in stages:
                ffn.final = store_tile_512
            ffn(0, after1)

        if "mix" in stages:
            if "ffn1" not in stages:
                load_mixer_weights()
                load_mixer_weights2()
            mixer()
            if "ffn2" not in stages:
                for t in range(NT):
                    store_tile_512(t)

        if "ffn2" in stages:
            if "ffn1" not in stages and "mix" not in stages:
                ffn.pre[0] = load_ffn_group(1, 0)
                ffn.pre[1] = load_ffn_group(1, 1)
            for t in range(NT):
                norm_tile(t, R_GF2, t, extra_w=(ALL_HTM if t == 0 else ()))
            ffn.final = store_tile_512
            ffn(1, None)

        if "ffn1" not in stages and "ffn2" not in stages and "mix" not in stages:
            for t in range(NT):
                store_tile_512(t)

        P.wait_all("sp", [("out", b) for b in range(NB)])
        P.emit()
    return nc


_INVF = (np.float32(500000.0) ** (-np.arange(8, dtype=np.float32) / np.float32(8))).astype(np.float32)


def _pack_ffn(wg, wu, wd):
    wg3 = wg.reshape(8, 128, DFF)
    wu3 = wu.reshape(8, 128, DFF)
    wd3 = wd.reshape(NFC, 128, D)
    parts = []
    for f0, G in GROUPS:
        cs_ = slice(f0 * 128, (f0 + G) * 128)
        parts.append(wg3[:, :, cs_].transpose(1, 0, 2).reshape(128, -1))
        parts.append(wu3[:, :, cs_].transpose(1, 0, 2).reshape(128, -1))
        parts.append(wd3[f0:f0 + G].transpose(1, 0, 2).reshape(128, -1))
    return np.ascontiguousarray(np.concatenate(parts, axis=1))


def make_in_maps(inputs):
    f = lambda a: np.ascontiguousarray(np.asarray(a))
    w_in = f(inputs["w_in"])[0]
    w_in_r = np.concatenate([w_in[:, 0:768], w_in[:, 1280:1536], w_in[:, 768:1280]], axis=1)
    wmix0 = np.ascontiguousarray(w_in_r.reshape(8, 128, 1536).transpose(1, 0, 2).reshape(128, 12288))
    w_out = f(inputs["w_out"])[0]
    wo = np.empty((128, 8, D), np.float32)
    wo[0:64, 0:4] = w_out[0:256].reshape(4, 64, D).transpose(1, 0, 2)
    wo[64:128, 0:4] = w_out[256:512].reshape(4, 64, D).transpose(1, 0, 2)
    wo[:, 4:8] = w_out[512:1024].reshape(4, 128, D).transpose(1, 0, 2)
    wmkv = f(inputs["w_mem_kv"])[0].reshape(8, 128, 512).transpose(1, 0, 2).reshape(128, 4096)
    wmix1 = np.ascontiguousarray(np.concatenate([wo.reshape(128, 8192), wmkv], axis=1))
    shared = {
        "invfreq": _INVF.reshape(1, 8),
        "g_ffn1": f(inputs["g_ffn1"]).reshape(8, 128), "g_ffn2": f(inputs["g_ffn2"]).reshape(8, 128),
        "wffn1": _pack_ffn(f(inputs["w_ffn1_gate"])[0], f(inputs["w_ffn1_up"])[0], f(inputs["w_ffn1_down"])[0]),
        "wffn2": _pack_ffn(f(inputs["w_ffn2_gate"])[0], f(inputs["w_ffn2_up"])[0], f(inputs["w_ffn2_down"])[0]),
        "wmix0": wmix0, "wmix1": wmix1,
        "g_mix": f(inputs["g_mix"]).reshape(8, 128),
        "g_q": f(inputs["g_q"]).reshape(1, 64), "g_k": f(inputs["g_k"]).reshape(1, 64),
        "sinks": f(inputs["sinks"]).reshape(2, 4), "w_dw": f(inputs["w_dw"])[0],
        "b_dw": f(inputs["b_dw"]).reshape(2, 128), "g_conv_ln": f(inputs["g_conv_ln"]).reshape(2, 128),
        "b_conv_ln": f(inputs["b_conv_ln"]).reshape(2, 128), "g_mem": f(inputs["g_mem"]).reshape(8, 128),
        "g_mq": f(inputs["g_mq"]).reshape(1, 64),
        "g_mk": f(inputs["g_mk"]).reshape(1, 64),
    }
    x = f(inputs["x"])
    mem = f(inputs["mem"])
    pos = f(inputs["positions"]).astype(np.int32)
    maps = []
    for b in range(8):
        m = dict(shared)
        m["x"] = x[b]
        m["mem"] = mem[b]
        m["positions"] = pos[b].reshape(NB, 128)
        maps.append(m)
    return maps


def kernel(**inputs):
    nc = build_nc()
    in_maps = make_in_maps(inputs)
    res = run_bass_kernel_spmd(nc, in_maps, core_ids=list(range(8)))
    return np.stack([r["out"] for r in res.results], axis=0).astype(np.float32)
```

```python
import math
import os
from contextlib import ExitStack

import numpy as np
import concourse.bass as bass
import concourse.mybir as mybir
from concourse.bass_utils import run_bass_kernel_spmd

F32 = mybir.dt.float32
BF16 = mybir.dt.bfloat16
I32 = mybir.dt.int32
ALU = mybir.AluOpType
AF = mybir.ActivationFunctionType
AX = mybir.AxisListType

T = 2048
D = 1024
DFF = 2816
NFC = DFF // 128
GROUPS = [(0, 4), (4, 4), (8, 4), (12, 4), (16, 3), (19, 3)]
EPS = 1e-6
NT = T // 512
NB = T // 128


class Prog:
    ENGS = ("pe", "act", "dve", "pool", "sp")

    def __init__(self, nc, stack):
        self.nc = nc
        self.sem = {}
        self.cnt = {}
        self.waited = {e: {} for e in self.ENGS}
        self.stream = {e: [] for e in self.ENGS}
        self.lastw = {}
        self.reads = {}
        self.stack = stack
        for e in self.ENGS:
            self.sem[e] = stack.enter_context(nc.semaphore("s_" + e))
            self.cnt[e] = 0

    def _need(self, eng, ev):
        if ev is None:
            return
        s, c = ev
        if s == eng and c > self.cnt[eng]:
            return
        if self.waited[eng].get(s, 0) >= c:
            return
        self.waited[eng][s] = c
        self.stream[eng].append(("wait", s, c))

    def _deps(self, eng, reads, writes):
        for k in reads:
            self._need(eng, self.lastw.get(k))
        for k in writes:
            self._need(eng, self.lastw.get(k))
            for s, c in self.reads.get(k, {}).items():
                self._need(eng, (s, c))

    def _record(self, ev, reads, writes):
        s, c = ev
        for k in reads:
            self.reads.setdefault(k, {})[s] = c
        for k in writes:
            self.lastw[k] = ev
            self.reads[k] = {}

    def op(self, eng, fn, reads=(), writes=(), inc=True):
        bank_r = [k for k in reads if isinstance(k, tuple) and k[0] == "bank"]
        if bank_r:
            reads = [k for k in reads if k not in bank_r]
            writes = list(writes) + bank_r
        self._deps(eng, reads, writes)
        if inc:
            self.cnt[eng] += 1
            ev = (eng, self.cnt[eng])
            self.stream[eng].append(("ins", fn, eng, 1))
        else:
            ev = (eng, self.cnt[eng] + 1)
            self.stream[eng].append(("ins", fn, None, 0))
        self._record(ev, reads, writes)

    def dma(self, q, semname, out, in_, reads=(), writes=(), deps=True, **kw):
        if semname not in self.sem:
            self.sem[semname] = self.stack.enter_context(self.nc.semaphore("d_" + semname))
            self.cnt[semname] = 0
        if deps:
            self._deps(q, reads, writes)
        self.cnt[semname] += 16
        ev = (semname, self.cnt[semname])
        self.stream[q].append(
            ("ins", lambda h: h.dma_start(out=out, in_=in_, **kw), semname, 16))
        self._record(ev, reads, writes)

    def wait_write(self, eng, keys):
        for k in keys:
            self._need(eng, self.lastw.get(k))

    def wait_all(self, eng, keys):
        for k in keys:
            self._need(eng, self.lastw.get(k))
            for s, c in self.reads.get(k, {}).items():
                self._need(eng, (s, c))

    def emit(self):
        nc = self.nc
        with nc.Block() as block:
            def run(e):
                def body(h):
                    for it in self.stream[e]:
                        if it[0] == "wait":
                            h.wait_ge(self.sem[it[1]], it[2])
                        else:
                            _, fn, s, n = it
                            ins = fn(h)
                            if s is not None:
                                ins.then_inc(self.sem[s], n)
                return body
            block.tensor(run("pe"))
            block.scalar(run("act"))
            block.vector(run("dve"))
            block.gpsimd(run("pool"))
            block.sync(run("sp"))


def build_nc(stages=("ffn1", "mix", "ffn2"), debug=False):
    nc = bass.Bass("TRN2", target_bir_lowering=False)

    def din(name, shape, dt=F32):
        return nc.dram_tensor(name, shape, dt, kind="ExternalInput").ap()

    x_d = din("x", [T, D])
    mem_d = din("mem", [256, D])
    pos_d = din("positions", [NB, 128], I32)
    invf_d = din("invfreq", [1, 8])
    g_ffn_d = [din("g_ffn1", [8, 128]), din("g_ffn2", [8, 128])]
    wffn_d = [din("wffn1", [128, NFC * 3072]), din("wffn2", [128, NFC * 3072])]
    wmix0_d = din("wmix0", [128, 12288])
    wmix1_d = din("wmix1", [128, 12288])
    g_mix_d = din("g_mix", [8, 128])
    g_q_d = din("g_q", [1, 64])
    g_k_d = din("g_k", [1, 64])
    sinks_d = din("sinks", [2, 4])
    w_dw_d = din("w_dw", [31, 256])
    b_dw_d = din("b_dw", [2, 128])
    g_cln_d = din("g_conv_ln", [2, 128])
    b_cln_d = din("b_conv_ln", [2, 128])
    g_mem_d = din("g_mem", [8, 128])
    g_mq_d = din("g_mq", [1, 64])
    g_mk_d = din("g_mk", [1, 64])
    out_d = nc.dram_tensor("out", [T, D], F32, kind="ExternalOutput").ap()

    with ExitStack() as st:
        P = Prog(nc, st)

        def sb(name, shape, dt):
            return st.enter_context(nc.sbuf_tensor(name, shape, dt))

        xT = sb("xT", [128, 8, T], F32)
        hT = sb("hT", [128, 8, T], BF16)
        ring = sb("ring", [128, 2, 12288], BF16)
        aT = sb("aT", [128, 2, 4, 512], BF16)
        kT = sb("kT", [128, T], BF16)
        vtok = sb("vtok", [128, NB, 128], BF16)
        mqT = sb("mqT", [128, 2, 2, 512], BF16)
        hglu = sb("hglu", [128, 2, 2, 32 + 512], BF16)
        catT = sb("catT", [128, 8, 512], BF16)
        xstage = sb("xstage", [128, 2, D], F32)
        PT = xstage[:, :, :].bitcast(BF16).rearrange("p b (s t) -> p b s t", s=4)
        mkT = sb("mkT", [128, 2, 256], BF16)
        mvtok = sb("mvtok", [128, 2, 256], BF16)
        ident_f = sb("ident_f", [128, 128], F32)
        ident_b = sb("ident_b", [128, 128], BF16)
        ones_d = sb("ones_d", [128, 128], BF16)
        ones_1 = sb("ones_1", [128, 64], BF16)
        colsT = sb("colsT", [128, 100], F32)
        posf = sb("posf", [128, NB], F32)
        gtmp = sb("gtmp", [128, 4, 64], F32)
        invf = sb("invf", [128, 8], F32)
        cs = sb("cs", [128, 2, NB, 8], F32)
        mask_b = sb("mask_b", [128, 2, 128], BF16)
        esink = sb("esink", [128, 4], F32)
        small = sb("small", [128, 16], F32)
        rstd = sb("rstd", [128, 1, 512], F32)
        sg = sb("sg", [128, 2, 512], F32)
        qk_f = sb("qk_f", [128, 16, 64], F32)
        mask_f = qk_f[:, 4:8, :].rearrange("p (a b) c -> p a (b c)", a=2)
        prow = qk_f[:, 0:2, :].rearrange("p a c -> p (a c)")
        qk_b = sb("qk_b", [128, 7, 128], BF16)
        rtmp = sb("rtmp", [128, 4, 10, 8], F32)
        pos_i = rtmp[0:NB, :, :, :].rearrange("p a b c -> p (a b c)")[:, 0:128].bitcast(I32)
        st16 = sb("st16", [128, 3, 16], F32)
        rs2t = sb("rs2t", [128, 512], F32)
        cv = sg
        cvb = sb("cvb", [128, 2, 512], BF16)

        banks = [st.enter_context(nc.psum_tensor("pb%d" % i, [128, 512], F32)) for i in range(8)]
        bk = lambda i: ("bank", i)

        P.op("pool", lambda h: h.memset(ident_f[:, :], 0.0), writes=["ident_f"])
        P.op("pool", lambda h: h.affine_select(out=ident_f[:, :], in_=ident_f[:, :], compare_op=ALU.not_equal,
                                                fill=1.0, base=0, pattern=[[-1, 128]], channel_multiplier=1),
             reads=["ident_f"], writes=["ident_f"])
        P.op("pool", lambda h: h.tensor_copy(out=ident_b[:, :], in_=ident_f[:, :]), reads=["ident_f"], writes=["ident_b"])
        DBG = os.environ.get("KDBG", "")
        if "noconst" not in DBG:
            P.op("pool", lambda h: h.memset(ones_d[:, :], 1.0 / 1024), writes=["ones_d"])
            P.op("pool", lambda h: h.memset(ones_1[:, :], 1.0), writes=["ones_1"])
            P.op("pool", lambda h: h.memset(small[:, :], 0.0), writes=["small"])
            P.op("pool", lambda h: h.memset(small[:, 0:1], -math.pi), reads=["small"], writes=["small"])
            P.op("pool", lambda h: h.memset(small[:, 1:2], -0.5), reads=["small"], writes=["small"])
            P.op("pool", lambda h: h.memset(small[:, 2:3], 2.0 * math.pi), reads=["small"], writes=["small"])
            P.op("pool", lambda h: h.memset(prow[:, :], 0.0), writes=["prow"])
            SQ = "sp"
            P.op("pool", lambda h: h.memset(hglu[:, 0, :, 0:32], 0.0), writes=[("hglu", 0, 0), ("hglu", 0, 1)])
            if "nomask" not in DBG:
              P.op("pool", lambda h: h.memset(mask_f[:, :, :], 0.0), writes=["mask_f"])
            P.op("pool", lambda h: h.affine_select(out=mask_f[:, 0, :], in_=mask_f[:, 0, :], compare_op=ALU.is_gt,
                                                    fill=-60000.0, base=0, pattern=[[-1, 128]], channel_multiplier=1),
                 reads=["mask_f"], writes=["mask_f"])
            P.op("pool", lambda h: h.affine_select(out=mask_f[:, 1, :], in_=mask_f[:, 1, :], compare_op=ALU.is_ge,
                                                    fill=-60000.0, base=0, pattern=[[1, 128]], channel_multiplier=-1),
                 reads=["mask_f"], writes=["mask_f"])
            P.op("pool", lambda h: h.tensor_copy(out=mask_b[:, :, :], in_=mask_f[:, :, :]), reads=["mask_f"], writes=["mask_b"])

            R_GF1, R_GMIX, R_GF2, R_GMEM, R_WDW, R_BDW, R_GCLN, R_BCLN = 0, 8, 16, 24, 32, 94, 96, 98
            small_loads = [(R_GF1, 8, g_ffn_d[0]), (R_GMIX, 8, g_mix_d), (R_GF2, 8, g_ffn_d[1]), (R_GMEM, 8, g_mem_d),
                           (R_WDW, 31, w_dw_d[:, 0:128]), (R_WDW + 31, 31, w_dw_d[:, 128:256]),
                           (R_BDW, 2, b_dw_d), (R_GCLN, 2, g_cln_d), (R_BCLN, 2, b_cln_d)]
            for r0, n, src in small_loads:
                P.dma(SQ, "small", prow[r0:r0 + n, :], src, reads=["prow"], writes=[("prow", r0)])
            P.dma(SQ, "small", pos_i[:, :], pos_d, writes=["pos_i"])
            P.dma(SQ, "small", invf[:, :], invf_d.partition_broadcast(128), writes=["invf"])
            P.dma(SQ, "small", gtmp[:, 0, :], g_q_d.partition_broadcast(128), writes=[("gtmp", len(P.stream[SQ]))])
            P.dma(SQ, "small", gtmp[:, 1, :], g_k_d.partition_broadcast(128), writes=[("gtmp", len(P.stream[SQ]))])
            P.dma(SQ, "small", gtmp[:, 2, :], g_mq_d.partition_broadcast(128), writes=[("gtmp", len(P.stream[SQ]))])
            P.dma(SQ, "small", gtmp[:, 3, :], g_mk_d.partition_broadcast(128), writes=[("gtmp", len(P.stream[SQ]))])
            P.dma(SQ, "small", esink[0:64, :], sinks_d[0:1, :].partition_broadcast(64), writes=[("esink", 0)])
            P.dma(SQ, "small", esink[64:128, :], sinks_d[1:2, :].partition_broadcast(64), writes=[("esink", 1)])

            for k, ev in list(P.lastw.items()):
                if ev[0] == "small":
                    P.lastw[k] = ("small", P.cnt["small"])

        def load_ffn_group(f, gi, after=(), part=0):
            f0, G = GROUPS[gi]
            if part == 2:
                slot = load_ffn_group.last
            else:
                slot = load_ffn_group.n % 2
                load_ffn_group.n += 1
                load_ffn_group.last = slot
            off = f0 * 3072
            gu = 16 * G * 128
            P.wait_write("pool", after)
            if part == 0:
                P.dma("pool", "ring%d" % slot, ring[:, slot, 0:G * 3072], wffn_d[f][:, off:off + G * 3072],
                      writes=[("ring", slot), ("ringd", slot)])
            elif part == 1:
                P.dma("pool", "ring%d" % slot, ring[:, slot, 0:gu], wffn_d[f][:, off:off + gu], writes=[("ring", slot)])
            else:
                P.dma("pool", "ringd%d" % slot, ring[:, slot, gu:G * 3072], wffn_d[f][:, off + gu:off + G * 3072],
                      writes=[("ringd", slot)])
            return slot
        load_ffn_group.n = 0
        load_ffn_group.last = 0

        def load_mixer_weights():
            P.dma("pool", "ring0", ring[:, 0, :], wmix0_d[:, :], writes=[("ring", 0), ("ringd", 0)])

        def load_mixer_weights2():
            P.dma("pool", "ring1", ring[:, 1, :], wmix1_d[:, :], writes=[("ring", 1), ("ringd", 1)])


        if "noconst" not in DBG:
            P.op("pe", lambda h: h.transpose(out=banks[7][:, 0:128], in_=prow[:, :], identity=ident_f[:, :]),
                 reads=[("prow", r0) for r0, _, _ in small_loads] + ["ident_f"], writes=[bk(7)])
            P.op("dve", lambda h: h.tensor_copy(out=colsT[:, :], in_=banks[7][:, 0:100]), reads=[bk(7)], writes=["colsT"])
        col = lambda r: colsT[:, r:r + 1]

        def load_x_tile(b):
            s = b % 2
            P.dma("sp", "xs%d" % s, xstage[:, s, :], x_d[b * 128:(b + 1) * 128, :], writes=[("xstage", s)])

        def transpose_x_tile(b):
            s = b % 2
            for half in range(2):
                bank = banks[(2 * b + half) % 4]
                key = bk((2 * b + half) % 4)
                for c4 in range(4):
                    c = half * 4 + c4
                    P.op("pe", lambda h, c=c, c4=c4, bank=bank: h.transpose(
                        out=bank[:, c4 * 128:(c4 + 1) * 128], in_=xstage[:, s, c * 128:(c + 1) * 128],
                        identity=ident_f[:, :]),
                        reads=[("xstage", s), "ident_f"], writes=[key], inc=(c4 == 3))
                src = bank[:, :].rearrange("p (c t) -> p c t", c=4)
                dst = xT[:, half * 4:half * 4 + 4, b * 128:(b + 1) * 128]
                wk = [("xT", c, b // 4) for c in range(half * 4, half * 4 + 4)]
                if half == 0:
                    P.op("act", lambda h, src=src, dst=dst: h.copy(out=dst, in_=src), reads=[key], writes=wk)
                else:
                    P.op("dve", lambda h, src=src, dst=dst: h.tensor_copy(out=dst, in_=src), reads=[key], writes=wk)

        hflat = hT[:, :, :].rearrange("p c t -> p (c t)")
        hTm = hflat[:, 0:8192].rearrange("p (b c t) -> p b c t", b=2, c=8)
        dstore = hflat[:, 8192:8192 + 62 * 128].rearrange("p (n m) -> p n m", n=62)
        ALL_HT = [("hT", c, t) for c in range(8) for t in range(NT)]
        ALL_HTM = [("hTm", b_, c) for b_ in range(2) for c in range(8)] + ["dstore"]

        def norm_tile(t, grow, nbuf, mbuf=None, extra_w=(), part=None):
            tok = slice(t * 512, (t + 1) * 512)
            xk = [("xT", c, t) for c in range(8)]
            if mbuf is None:
                dst = hT[:, :, tok]
                hkey = lambda c: ("hT", c, t)
                pbank = 6 + (nbuf % 2)
            else:
                dst = hTm[:, mbuf, :, :]
                hkey = lambda c: ("hTm", mbuf, c)
                pbank = 7
            hk = [hkey(c) for c in range(8)]
            if part in (None, "sq"):
                P.op("act", lambda h: h.activation(out=dst, in_=xT[:, :, tok], func=AF.Square),
                     reads=xk, writes=hk + list(extra_w))
            if part == "sq":
                return
            for c in range(8):
                P.op("pe", lambda h, c=c: h.matmul(banks[pbank][:, :], lhsT=ones_d[:, :], rhs=dst[:, c, :],
                                                   start=(c == 0), stop=(c == 7)),
                     reads=[hkey(c), "ones_d"], writes=[bk(pbank)], inc=(c == 7))
            i = 0
            P.op("dve", lambda h: h.tensor_scalar_add(out=rstd[:, i, :], in0=banks[pbank][:, :], scalar1=EPS),
                 reads=[bk(pbank)], writes=[("rstd", i)])
            P.op("act", lambda h: h.activation(out=rstd[:, i, :], in_=rstd[:, i, :], func=AF.Ln),
                 reads=[("rstd", i)], writes=[("rstd", i)])
            P.op("act", lambda h: h.activation(out=rstd[:, i, :], in_=rstd[:, i, :], func=AF.Exp, scale=-0.5),
                 reads=[("rstd", i)], writes=[("rstd", i)])
            for c in range(8):
                P.op("dve", lambda h, c=c: h.scalar_tensor_tensor(
                    out=dst[:, c, :], in0=xT[:, c, tok], scalar=col(grow + c), in1=rstd[:, i, :],
                    op0=ALU.mult, op1=ALU.mult),
                    reads=[("xT", c, t), "colsT", ("rstd", i)], writes=[hkey(c)])

        def ffn(f, after_stage=None):
            stagesl = [(gi, t) for gi in range(len(GROUPS)) for t in range(NT)]
            slots = {}
            slots[0] = 0 if f == 0 else ffn.pre[0]
            slots[1] = 1 if f == 0 else ffn.pre[1]

            def GU(i):
                gi, t = stagesl[i]
                f0, G = GROUPS[gi]
                slot = slots[gi]
                tok = slice(t * 512, (t + 1) * 512)
                ab = i % 2
                gw = G * 128
                wgv = ring[:, slot, 0:8 * gw].rearrange("p (k f) -> p k f", k=8)
                wuv = ring[:, slot, 8 * gw:16 * gw].rearrange("p (k f) -> p k f", k=8)
                for fi in range(G):
                    gb = (GU.n % 2)
                    ub = 2 + (GU.n % 2)
                    GU.n += 1
                    for (bnk, wv) in ((gb, wgv), (ub, wuv)):
                        for kc in range(8):
                            P.op("pe", lambda h, bnk=bnk, wv=wv, kc=kc, fi=fi: h.matmul(
                                banks[bnk][:, :], lhsT=wv[:, kc, fi * 128:(fi + 1) * 128], rhs=hT[:, kc, tok],
                                start=(kc == 0), stop=(kc == 7)),
                                reads=[("ring", slot), ("hT", kc, t)], writes=[bk(bnk)], inc=(kc == 7))
                    sgi = gb
                    P.op("act", lambda h, gb=gb, sgi=sgi: h.activation(out=sg[:, sgi, :], in_=banks[gb][:, :], func=AF.Silu),
                         reads=[bk(gb)], writes=[("sg", sgi)])
                    P.op("dve", lambda h, ub=ub, sgi=sgi, fi=fi: h.tensor_tensor(
                        out=aT[:, ab, fi, :], in0=sg[:, sgi, :], in1=banks[ub][:, :], op=ALU.mult),
                        reads=[("sg", sgi), bk(ub)], writes=[("aT", ab, fi)])
            GU.n = 0

            def Dn(i):
                gi, t = stagesl[i]
                f0, G = GROUPS[gi]
                slot = slots[gi]
                tok = slice(t * 512, (t + 1) * 512)
                ab = i % 2
                wdv = ring[:, slot, 16 * G * 128:16 * G * 128 + G * 1024].rearrange("p (g d) -> p g d", g=G)
                for dc in range(8):
                    bnk = 4 + (Dn.n % 2)
                    Dn.n += 1
                    for fi in range(G):
                        P.op("pe", lambda h, bnk=bnk, fi=fi, dc=dc: h.matmul(
                            banks[bnk][:, :], lhsT=wdv[:, fi, dc * 128:(dc + 1) * 128], rhs=aT[:, ab, fi, :],
                            start=(fi == 0), stop=(fi == G - 1)),
                            reads=[("ringd", slot), ("aT", ab, fi)], writes=[bk(bnk)], inc=(fi == G - 1))
                    P.op("dve", lambda h, bnk=bnk, dc=dc: h.scalar_tensor_tensor(
                        out=xT[:, dc, tok], in0=banks[bnk][:, :], scalar=0.5, in1=xT[:, dc, tok],
                        op0=ALU.mult, op1=ALU.add),
                        reads=[bk(bnk), ("xT", dc, t)], writes=[("xT", dc, t)])
                if t == NT - 1:
                    nxt = gi + 2
                    if nxt < len(GROUPS):
                        slots[nxt] = load_ffn_group(f, nxt)
                    elif after_stage is not None:
                        after_stage(gi)
                if gi == len(GROUPS) - 1 and ffn.final is not None:
                    ffn.final(t)
            Dn.n = 0

            for i in range(len(stagesl)):
                GU(i)
                if i >= 1:
                    Dn(i - 1)
            Dn(len(stagesl) - 1)
        ffn.pre = {}
        ffn.final = None

        def mixer():
            win = ring[:, 0, :].rearrange("p (k f) -> p k f", k=8)
            wout = ring[:, 1, 0:8192].rearrange("p (c d) -> p c d", c=8)
            wmkv = ring[:, 1, 8192:12288].rearrange("p (k f) -> p k f", k=8)
            qkf2 = qk_f[:, :, :].rearrange("p a c -> p (a c)")
            qkb2 = qk_b[:, :, :].rearrange("p a c -> p (a c)")
            memn = cvb[:, :, :].rearrange("p a (c t) -> p (a c) t", c=4)
            memT = catT[:, :, 0:256]
            b2v = banks[2][:, :].bitcast(BF16)
            b6v = banks[6][:, :].bitcast(BF16)
            rs = rstd[:, 0, :]
            rs2 = rs2t[:, :]
            RK = ("rstd", 0)
            negm_swa = small[:, 3:4]
            negm_mem = small[:, 4:5]

            P.op("dve", lambda h: h.reduce_max(out=st16[:, 0, 0:4], in_=gtmp[:, :, :], axis=AX.X, apply_absolute_value=True),
                 reads=[k for k in P.lastw if isinstance(k, tuple) and k[0] == "gtmp"], writes=["st16a"])
            P.op("dve", lambda h: h.tensor_tensor(out=small[:, 3:4], in0=st16[:, 0, 0:1], in1=st16[:, 0, 1:2], op=ALU.mult),
                 reads=["st16a", "small"], writes=["small"])
            P.op("dve", lambda h: h.tensor_tensor(out=small[:, 4:5], in0=st16[:, 0, 2:3], in1=st16[:, 0, 3:4], op=ALU.mult),
                 reads=["st16a", "small"], writes=["small"])
            P.op("dve", lambda h: h.tensor_scalar_mul(out=small[:, 3:5], in0=small[:, 3:5], scalar1=-8.0),
                 reads=["small"], writes=["small"])
            P.op("act", lambda h: h.activation(out=esink[:, :], in_=esink[:, :], func=AF.Exp, bias=negm_swa, scale=1.0),
                 reads=[("esink", 0), ("esink", 1), "small"], writes=["esinkx"])

            P.op("dve", lambda h: h.memset(st16[:, 1, :], 0.0), writes=["st16b"])
            for mt in range(2):
                P.op("act", lambda h, mt=mt: h.activation(out=qkf2, in_=xstage[:, mt, :], func=AF.Square,
                                                           accum_out=st16[:, 1, mt:mt + 1]),
                     reads=[("xstage", mt), "st16b"],
                     writes=["qk_f", "st16b"] + (["mask_f", "prow", "prow2", "ang"] + [("prow", r0) for r0, _, _ in small_loads]
                                                 if mt == 0 else []))
                P.op("dve", lambda h, mt=mt: h.tensor_scalar(out=st16[:, 1, mt:mt + 1], in0=st16[:, 1, mt:mt + 1],
                                                              scalar1=1.0 / 1024, scalar2=EPS, op0=ALU.mult, op1=ALU.add),
                     reads=["st16b"], writes=["st16b"])
                P.op("act", lambda h, mt=mt: h.activation(out=st16[:, 1, mt:mt + 1], in_=st16[:, 1, mt:mt + 1], func=AF.Ln),
                     reads=["st16b"], writes=["st16b"])
                P.op("act", lambda h, mt=mt: h.activation(out=st16[:, 1, mt:mt + 1], in_=st16[:, 1, mt:mt + 1], func=AF.Exp, scale=-0.5),
                     reads=["st16b"], writes=["st16b"])
                P.op("act", lambda h, mt=mt: h.activation(out=cvb[:, :, :].rearrange("p a t -> p (a t)"), in_=xstage[:, mt, :],
                                                           func=AF.Copy, scale=st16[:, 1, mt:mt + 1]),
                     reads=[("xstage", mt), "st16b"], writes=["cvb"])
                for c in range(8):
                    P.op("pe", lambda h, c=c: h.transpose(out=b6v[:, c * 128:(c + 1) * 128], in_=memn[:, c, :],
                                                          identity=ident_b[:, :]),
                         reads=["cvb", "ident_b"], writes=[bk(6)], inc=(c == 7))
                P.op("dve", lambda h, mt=mt: h.tensor_tensor(
                    out=memT[:, :, mt * 128:(mt + 1) * 128], in0=b6v.rearrange("p (c t) -> p c t", c=8),
                    in1=colsT[:, R_GMEM:R_GMEM + 8].unsqueeze(2).to_broadcast([128, 8, 128]), op=ALU.mult),
                    reads=[bk(6), "colsT"], writes=["catT_mem"])
            def mem_prep2():
                for mt in range(2):
                    for kc in range(8):
                        P.op("pe", lambda h, kc=kc, mt=mt: h.matmul(banks[7][:, :], lhsT=memT[:, kc, mt * 128:(mt + 1) * 128],
                                                                    rhs=wmkv[:, kc, :], start=(kc == 0), stop=(kc == 7)),
                             reads=["catT_mem", ("ring", 1)], writes=[bk(7)], inc=(kc == 7))
                    P.op("act", lambda h: h.activation(out=qkf2[:, 0:256], in_=banks[7][:, 0:256], func=AF.Square),
                         reads=[bk(7)], writes=["qk_f"])
                    P.op("dve", lambda h: h.reduce_sum(out=st16[:, 2, 0:4], in_=qk_f[:, 0:4, :], axis=AX.X),
                         reads=["qk_f"], writes=["st16c"])
                    P.op("dve", lambda h: h.tensor_scalar(out=st16[:, 2, 0:4], in0=st16[:, 2, 0:4], scalar1=1.0 / 64, scalar2=EPS,
                                                           op0=ALU.mult, op1=ALU.add), reads=["st16c"], writes=["st16c"])
                    P.op("act", lambda h: h.activation(out=st16[:, 2, 0:4], in_=st16[:, 2, 0:4], func=AF.Ln),
                         reads=["st16c"], writes=["st16c"])
                    P.op("act", lambda h: h.activation(out=st16[:, 2, 0:4], in_=st16[:, 2, 0:4], func=AF.Exp, scale=-0.5),
                         reads=["st16c"], writes=["st16c"])
                    P.op("dve", lambda h: h.tensor_tensor(out=qk_f[:, 0:4, :], in0=banks[7][:, 0:256].rearrange("p (a c) -> p a c", a=4),
                                                          in1=gtmp[:, 3:4, :].to_broadcast([128, 4, 64]), op=ALU.mult),
                         reads=[bk(7), "qk_f"], writes=["qk_f"])
                    P.op("dve", lambda h: h.tensor_tensor(out=qk_b[:, 0:2, :].rearrange("p a (b c) -> p (a b) c", b=2),
                                                          in0=qk_f[:, 0:4, :],
                                                          in1=st16[:, 2, 0:4].unsqueeze(2).to_broadcast([128, 4, 64]), op=ALU.mult),
                         reads=["qk_f", "st16c"], writes=["qk_b"])
                    P.op("act", lambda h, mt=mt: h.copy(out=mvtok[:, mt, :], in_=banks[7][:, 256:512]),
                         reads=[bk(7)], writes=["mvtok"])
                    for j in range(2):
                        P.op("pe", lambda h, j=j: h.transpose(out=b6v[:, j * 128:(j + 1) * 128], in_=qk_b[:, j, :],
                                                              identity=ident_b[:, :]),
                             reads=["qk_b", "ident_b"], writes=[bk(6)], inc=(j == 1))
                    P.op("dve", lambda h, mt=mt: h.tensor_copy(out=mkT[:, :, mt * 128:(mt + 1) * 128],
                                                                in_=b6v[:, 0:256].rearrange("p (c t) -> p c t", c=2)),
                         reads=[bk(6)], writes=["mkT"])


            stt_cnt = [0]

            def st_bank():
                stt_cnt[0] += 1
                return 5 + (stt_cnt[0] % 2)

            def proj_mm(b):
                Tt = b // 4
                mb = Tt % 2
                l128 = slice((b % 4) * 128, (b % 4 + 1) * 128)
                for kc in range(8):
                    P.op("pe", lambda h, kc=kc: h.matmul(banks[0][:, :], lhsT=hTm[:, mb, kc, l128], rhs=win[:, kc, 0:512],
                                                         start=(kc == 0), stop=(kc == 7)),
                         reads=[("hTm", mb, kc), ("ring", 0)], writes=[bk(0)], inc=(kc == 7))
                for kc in range(8):
                    P.op("pe", lambda h, kc=kc: h.matmul(banks[1][:, :], lhsT=hTm[:, mb, kc, l128], rhs=win[:, kc, 512:1024],
                                                         start=(kc == 0), stop=(kc == 7)),
                         reads=[("hTm", mb, kc), ("ring", 0)], writes=[bk(1)], inc=(kc == 7))

            sqb = xstage[:, 1, :]

            def normrope_a(b):
                P.op("act", lambda h: h.activation(out=sqb[:, 0:512], in_=banks[0][:, :], func=AF.Square),
                     reads=[bk(0)], writes=[("xstage", 1)])
                P.op("act", lambda h: h.activation(out=sqb[:, 512:1024], in_=banks[1][:, :], func=AF.Square),
                     reads=[bk(1)], writes=[("xstage", 1)])

            def normrope_b(b):
                P.op("dve", lambda h: h.reduce_sum(out=st16[:, 0, :], in_=sqb.rearrange("p (a d) -> p a d", d=64), axis=AX.X),
                     reads=[("xstage", 1)], writes=["st16a"])
                P.op("dve", lambda h: h.tensor_scalar(out=st16[:, 0, :], in0=st16[:, 0, :], scalar1=1.0 / 64, scalar2=EPS,
                                                       op0=ALU.mult, op1=ALU.add), reads=["st16a"], writes=["st16a"])
                P.op("act", lambda h: h.activation(out=st16[:, 0, :], in_=st16[:, 0, :], func=AF.Ln),
                     reads=["st16a"], writes=["st16a"])
                P.op("act", lambda h: h.activation(out=st16[:, 0, :], in_=st16[:, 0, :], func=AF.Exp, scale=-0.5),
                     reads=["st16a"], writes=["st16a"])
                P.op("dve", lambda h: h.tensor_tensor(out=qk_f[:, 0:8, :], in0=banks[0][:, :].rearrange("p (a c) -> p a c", a=8),
                                                      in1=gtmp[:, 0:1, :].to_broadcast([128, 8, 64]), op=ALU.mult),
                     reads=[bk(0), "qk_f"], writes=["qk_f"])
                P.op("dve", lambda h: h.tensor_tensor(out=qk_f[:, 8:10, :], in0=banks[1][:, 0:128].rearrange("p (a c) -> p a c", a=2),
                                                      in1=gtmp[:, 1:2, :].to_broadcast([128, 2, 64]), op=ALU.mult),
                     reads=[bk(1), "qk_f"], writes=["qk_f"])
                P.op("dve", lambda h: h.tensor_tensor(out=qk_f[:, 12:16, :], in0=banks[1][:, 256:512].rearrange("p (a c) -> p a c", a=4),
                                                      in1=gtmp[:, 2:3, :].to_broadcast([128, 4, 64]), op=ALU.mult),
                     reads=[bk(1), "qk_f"], writes=["qk_f"])
                P.op("act", lambda h: h.copy(out=vtok[:, b, :], in_=banks[1][:, 128:256]),
                     reads=[bk(1)], writes=[("vtok", b)])
                x1 = qk_f[:, 0:10, 0:8]
                x2 = qk_f[:, 0:10, 8:16]
                cosb = cs[:, 1, b, :].unsqueeze(1).to_broadcast([128, 10, 8])
                sinb = cs[:, 0, b, :].unsqueeze(1).to_broadcast([128, 10, 8])
                for i_, (xa, tb) in enumerate(((x1, cosb), (x2, sinb), (x2, cosb), (x1, sinb))):
                    P.op("dve", lambda h, i_=i_, xa=xa, tb=tb: h.tensor_tensor(out=rtmp[:, i_, :, :], in0=xa, in1=tb, op=ALU.mult),
                         reads=["qk_f", "cs"], writes=["rtmp", "pos_i"])
                P.op("dve", lambda h: h.tensor_tensor(out=x1, in0=rtmp[:, 0, :, :], in1=rtmp[:, 1, :, :], op=ALU.subtract),
                     reads=["rtmp", "qk_f"], writes=["qk_f"])
                P.op("dve", lambda h: h.tensor_tensor(out=x2, in0=rtmp[:, 2, :, :], in1=rtmp[:, 3, :, :], op=ALU.add),
                     reads=["rtmp", "qk_f"], writes=["qk_f"])
                P.op("dve", lambda h: h.tensor_tensor(
                    out=qk_b[:, 0:4, :].rearrange("p c (g d) -> p g c d", g=2),
                    in0=qk_f[:, 0:8, :].rearrange("p (g c) d -> p g c d", g=2),
                    in1=st16[:, 0, 0:8].rearrange("p (g c) -> p g c", g=2).unsqueeze(3).to_broadcast([128, 2, 4, 64]),
                    op=ALU.mult), reads=["qk_f", "st16a"], writes=["qk_b"])
                P.op("dve", lambda h: h.tensor_tensor(
                    out=qk_b[:, 4, :].rearrange("p (a d) -> p a d", a=2), in0=qk_f[:, 8:10, :],
                    in1=st16[:, 0, 8:10].unsqueeze(2).to_broadcast([128, 2, 64]), op=ALU.mult),
                    reads=["qk_f", "st16a"], writes=["qk_b"])
                P.op("dve", lambda h: h.tensor_tensor(
                    out=qk_b[:, 5:7, :].rearrange("p a (b d) -> p (a b) d", b=2), in0=qk_f[:, 12:16, :],
                    in1=st16[:, 0, 12:16].unsqueeze(2).to_broadcast([128, 4, 64]), op=ALU.mult),
                    reads=["qk_f", "st16a"], writes=["qk_b"])

            def tr_evac(b):
                Tt, s_ = b // 4, b % 4
                qb = Tt % 2
                t128 = slice(b * 128, (b + 1) * 128)
                c128 = slice(s_ * 128, (s_ + 1) * 128)
                for j in range(7):
                    P.op("pe", lambda h, j=j: h.transpose(out=b2v[:, j * 128:(j + 1) * 128], in_=qk_b[:, j, :],
                                                          identity=ident_b[:, :]),
                         reads=["qk_b", "ident_b"], writes=[bk(2)], inc=(j == 6))
                P.op("act", lambda h: h.copy(out=aT[:, qb, :, c128], in_=b2v[:, 0:512].rearrange("p (c t) -> p c t", c=4)),
                     reads=[bk(2)], writes=[("aT", qb, c) for c in range(4)])
                P.op("dve", lambda h: h.tensor_copy(out=kT[:, t128], in_=b2v[:, 512:640]),
                     reads=[bk(2)], writes=[("kT", b)])
                P.op("dve", lambda h: h.tensor_copy(out=mqT[:, qb, :, c128],
                                                    in_=b2v[:, 640:896].rearrange("p (c t) -> p c t", c=2)),
                     reads=[bk(2)], writes=[("mqT", qb, 0), ("mqT", qb, 1)])

            def glu(Tt):
                hb = Tt % 2
                mb = Tt % 2
                for j in range(2):
                    for kc in range(8):
                        P.op("pe", lambda h, j=j, kc=kc: h.matmul(banks[7][:, :], lhsT=win[:, kc, 1280 + j * 128:1280 + (j + 1) * 128],
                                                                  rhs=hTm[:, mb, kc, :], start=(kc == 0), stop=(kc == 7)),
                             reads=[("hTm", mb, kc), ("ring", 0)], writes=[bk(7)], inc=(kc == 7))
                    P.op("act", lambda h, j=j: h.activation(out=sg[:, j, :], in_=banks[7][:, :], func=AF.Sigmoid),
                         reads=[bk(7)], writes=[("sg", j)])
                    for kc in range(8):
                        P.op("pe", lambda h, j=j, kc=kc: h.matmul(banks[2][:, :], lhsT=win[:, kc, 1024 + j * 128:1024 + (j + 1) * 128],
                                                                  rhs=hTm[:, mb, kc, :], start=(kc == 0), stop=(kc == 7)),
                             reads=[("hTm", mb, kc), ("ring", 0)], writes=[bk(2)], inc=(kc == 7))
                    P.op("dve", lambda h, j=j: h.tensor_tensor(out=hglu[:, hb, j, 32:544], in0=sg[:, j, :], in1=banks[2][:, :], op=ALU.mult),
                         reads=[("sg", j), bk(2)], writes=[("hglu", hb, j)])

            def attn1(b):
                Tt, s_ = b // 4, b % 4
                qb = Tt % 2
                pb = 0
                c128 = slice(s_ * 128, (s_ + 1) * 128)
                kts = [1] if b == 0 else [0, 1]
                for g in range(2):
                    gp = slice(g * 64, (g + 1) * 64)
                    for kt in kts:
                        kb = b - 1 + kt
                        sbk = st_bank()
                        P.op("pe", lambda h, sbk=sbk, gp=gp, kb=kb: h.matmul(
                            banks[sbk][:, :], lhsT=kT[gp, kb * 128:(kb + 1) * 128], rhs=aT[gp, qb, :, c128],
                            start=True, stop=False),
                            reads=[("kT", kb)] + [("aT", qb, c) for c in range(4)], writes=[bk(sbk)], inc=False)
                        P.op("pe", lambda h, sbk=sbk, kt=kt: h.matmul(
                            banks[sbk][:, :], lhsT=ident_b[:, :], rhs=mask_b[:, kt, :].unsqueeze(1).to_broadcast([128, 4, 128]),
                            start=False, stop=True),
                            reads=["ident_b", "mask_b"], writes=[bk(sbk)])
                        idx = kt * 2 + g
                        P.op("act", lambda h, sbk=sbk, idx=idx: h.activation(
                            out=PT[:, pb, idx, :], in_=banks[sbk][:, :], func=AF.Exp, bias=negm_swa, scale=0.125),
                            reads=[bk(sbk), "small"], writes=[("PT", pb, idx)])
                for (bn, isden) in ((3, False), (4, True)):
                    for g in range(2):
                        gp = slice(g * 64, (g + 1) * 64)
                        for kt in kts:
                            kb = b - 1 + kt
                            idx = kt * 2 + g
                            lhs = ones_1[:, 0:64] if isden else vtok[:, kb, gp]
                            P.op("pe", lambda h, bn=bn, gp=gp, lhs=lhs, idx=idx, kt=kt: h.matmul(
                                banks[bn][gp, :], lhsT=lhs, rhs=PT[:, pb, idx, :], start=(kt == kts[0]), stop=(kt == 1)),
                                reads=[("vtok", kb), "ones_1", ("PT", pb, idx)], writes=[bk(bn)],
                                inc=(g == 1 and kt == 1))

            def attn2(b):
                Tt, s_ = b // 4, b % 4
                c128 = slice(s_ * 128, (s_ + 1) * 128)
                P.op("dve", lambda h: h.tensor_tensor(out=rs2.rearrange("p (c t) -> p c t", c=4),
                                                      in0=banks[4][:, :].rearrange("p (c t) -> p c t", c=4),
                                                      in1=esink[:, :].unsqueeze(2).to_broadcast([128, 4, 128]), op=ALU.add),
                     reads=[bk(4), "esinkx"], writes=["rs2"])
                P.op("act", lambda h: h.activation(out=rs2, in_=rs2, func=AF.Ln), reads=["rs2"], writes=["rs2"])
                P.op("act", lambda h: h.activation(out=rs2, in_=rs2, func=AF.Exp, scale=-1.0), reads=["rs2"], writes=["rs2"])
                P.op("dve", lambda h: h.tensor_tensor(out=catT[:, 0:4, c128],
                                                      in0=banks[3][:, :].rearrange("p (c t) -> p c t", c=4),
                                                      in1=rs2.rearrange("p (c t) -> p c t", c=4), op=ALU.mult),
                     reads=[bk(3), "rs2"], writes=[("catT", c) for c in range(4)] + (["catT_mem"] if Tt == 0 else []))

            def conv_a(Tt):
                hb = Tt % 2
                if Tt >= 1:
                    P.op("pool", lambda h: h.tensor_copy(out=hglu[:, hb, :, 2:32], in_=hglu[:, 1 - hb, :, 514:544]),
                         reads=[("hglu", 1 - hb, 0), ("hglu", 1 - hb, 1)], writes=[("hglu", hb, 0), ("hglu", hb, 1)])
                for j in range(2):
                    bn = 7
                    for tap in range(31):
                        P.op("pe", lambda h, j=j, tap=tap, bn=bn: h.matmul(
                            banks[bn][:, :], lhsT=dstore[:, j * 31 + tap, :], rhs=hglu[:, hb, j, 2 + tap:2 + tap + 512],
                            start=(tap == 0), stop=(tap == 30)),
                            reads=["dstore", ("hglu", hb, j)], writes=[bk(bn)], inc=(tap == 30))
                    P.op("act", lambda h, j=j, bn=bn: h.activation(out=cv[:, j, :], in_=banks[bn][:, :], func=AF.Identity,
                                                                    bias=col(R_BDW + j), scale=1.0),
                         reads=[bk(bn), "colsT"], writes=[("sg", j)])
                P.op("pool", lambda h: h.tensor_copy(out=cvb[:, :, :], in_=cv[:, :, :]), reads=[("sg", 0), ("sg", 1)], writes=["cvb"])

            def conv_b(Tt):
                for j in range(2):
                    P.op("pe", lambda h, j=j: h.matmul(banks[7][:, :], lhsT=ones_d[:, :], rhs=cvb[:, j, :], start=(j == 0), stop=(j == 1)),
                         reads=["cvb", "ones_d"], writes=[bk(7)], inc=(j == 1))
                for j in range(2):
                    P.op("dve", lambda h, j=j: h.scalar_tensor_tensor(out=cv[:, j, :], in0=banks[7][:, :], scalar=-4.0, in1=cv[:, j, :],
                                                                      op0=ALU.mult, op1=ALU.add),
                         reads=[("sg", j), bk(7)], writes=[("sg", j)])
                P.op("act", lambda h: h.activation(out=cvb[:, :, :], in_=cv[:, :, :], func=AF.Square),
                     reads=[("sg", 0), ("sg", 1)], writes=["cvb"])

            def conv_c(Tt):
                for j in range(2):
                    P.op("pe", lambda h, j=j: h.matmul(banks[7][:, :], lhsT=ones_d[:, :], rhs=cvb[:, j, :], start=(j == 0), stop=(j == 1)),
                         reads=["cvb", "ones_d"], writes=[bk(7)], inc=(j == 1))
                P.op("dve", lambda h: h.tensor_scalar(out=rs, in0=banks[7][:, :], scalar1=4.0, scalar2=EPS, op0=ALU.mult, op1=ALU.add),
                     reads=[bk(7)], writes=[RK])
                P.op("act", lambda h: h.activation(out=rs, in_=rs, func=AF.Ln), reads=[RK], writes=[RK])
                P.op("act", lambda h: h.activation(out=rs, in_=rs, func=AF.Exp, scale=-0.5), reads=[RK], writes=[RK])
                for j in range(2):
                    P.op("dve", lambda h, j=j: h.tensor_tensor(out=cv[:, j, :], in0=cv[:, j, :], in1=rs, op=ALU.mult),
                         reads=[("sg", j), RK], writes=[("sg", j)])
                    P.op("act", lambda h, j=j: h.activation(out=catT[:, 4 + j, :], in_=cv[:, j, :], func=AF.Silu,
                                                             bias=col(R_BCLN + j), scale=col(R_GCLN + j)),
                         reads=[("sg", j), "colsT"], writes=[("catT", 4 + j)] + (["catT_mem"] if Tt == 0 else []))

            def memattn(Tt, j):
                mqb = Tt % 2
                if True:
                    pb = 0
                    for half in range(2):
                        hp = slice(half * 64, (half + 1) * 64)
                        for mt in range(2):
                            sbk = st_bank()
                            idx = half * 2 + mt
                            P.op("pe", lambda h, sbk=sbk, hp=hp, mt=mt: h.matmul(
                                banks[sbk][:, :], lhsT=mkT[hp, j, mt * 128:(mt + 1) * 128], rhs=mqT[hp, mqb, j, :],
                                start=True, stop=True),
                                reads=["mkT", ("mqT", mqb, j)], writes=[bk(sbk)])
                            P.op("act", lambda h, sbk=sbk, idx=idx: h.activation(
                                out=PT[:, pb, idx, :], in_=banks[sbk][:, :], func=AF.Exp, bias=negm_mem, scale=0.125),
                                reads=[bk(sbk), "small"], writes=[("PT", pb, idx)])
                    for (bn, isden) in ((3, False), (4, True)):
                        for half in range(2):
                            hp = slice(half * 64, (half + 1) * 64)
                            hh = 2 * j + half
                            for mt in range(2):
                                idx = half * 2 + mt
                                lhs = ones_1[:, 0:64] if isden else mvtok[:, mt, hh * 64:(hh + 1) * 64]
                                P.op("pe", lambda h, bn=bn, hp=hp, lhs=lhs, idx=idx, mt=mt: h.matmul(
                                    banks[bn][hp, :], lhsT=lhs, rhs=PT[:, pb, idx, :], start=(mt == 0), stop=(mt == 1)),
                                    reads=["mvtok", "ones_1", ("PT", pb, idx)], writes=[bk(bn)],
                                    inc=(half == 1 and mt == 1))
                    P.op("act", lambda h: h.activation(out=rs2, in_=banks[4][:, :], func=AF.Ln), reads=[bk(4)], writes=["rs2"])
                    P.op("act", lambda h: h.activation(out=rs2, in_=rs2, func=AF.Exp, scale=-1.0), reads=["rs2"], writes=["rs2"])
                    P.op("dve", lambda h: h.tensor_tensor(out=catT[:, 6 + j, :], in0=banks[3][:, :], in1=rs2, op=ALU.mult),
                         reads=[bk(3), "rs2"], writes=[("catT", 6 + j)] + (["catT_mem"] if Tt == 0 else []))

            def oproj(Tt, c0, c1):
                tok = slice(Tt * 512, (Tt + 1) * 512)
                for dc in range(8):
                    bn = 7 if dc % 2 == 0 else 2
                    for c in range(c0, c1):
                        P.op("pe", lambda h, bn=bn, c=c, dc=dc: h.matmul(banks[bn][:, :], lhsT=wout[:, c, dc * 128:(dc + 1) * 128],
                                                                         rhs=catT[:, c, :], start=(c == c0), stop=(c == c1 - 1)),
                             reads=[("ring", 1), ("catT", c)], writes=[bk(bn)], inc=(c == c1 - 1))
                    P.op("dve", lambda h, bn=bn, dc=dc: h.tensor_tensor(out=xT[:, dc, tok], in0=xT[:, dc, tok], in1=banks[bn][:, :], op=ALU.add),
                         reads=[bk(bn), ("xT", dc, Tt)], writes=[("xT", dc, Tt)])

            for n in range(62):
                P.op("pool", lambda h, n=n: h.tensor_tensor(out=dstore[:, n, :], in0=ident_b[:, :],
                                                            in1=colsT[:, R_WDW + n:R_WDW + n + 1].to_broadcast([128, 128]), op=ALU.mult),
                     reads=["ident_b", "colsT"], writes=["dstore"] + (ALL_HT if n == 0 else []))
            norm_tile(0, R_GMIX, 0, mbuf=0, extra_w=ALL_HT)
            norm_tile(1, R_GMIX, 1, mbuf=1)
            proj_mm(0)
            normrope_a(0)
            for b in range(NB):
                pt, ph = (b // 4 - 1, b % 4) if b >= 4 else (None, None)
                if b >= 1:
                    attn1(b - 1)
                if pt is not None and ph == 0:
                    conv_b(pt)
                normrope_b(b)
                if b + 1 < NB:
                    proj_mm(b + 1)
                    normrope_a(b + 1)
                if pt is not None and ph == 0:
                    conv_c(pt)
                if b >= 1:
                    attn2(b - 1)
                if pt is not None:
                    if ph == 0:
                        oproj(pt, 0, 4)
                    elif ph == 1:
                        memattn(pt, 0)
                    elif ph == 2:
                        memattn(pt, 1)
                    elif ph == 3:
                        oproj(pt, 4, 8)
                tr_evac(b)
                if b == 0:
                    mem_prep2()
                if b % 4 == 3:
                    Tt = b // 4
                    glu(Tt)
                    if Tt + 2 < NT:
                        norm_tile(Tt + 2, R_GMIX, 0, mbuf=Tt % 2, part="sq")
                    if b == NB - 1 and "ffn2" in stages:
                        ffn.pre[0] = load_ffn_group(1, 0)
                    conv_a(Tt)
                    if Tt + 2 < NT:
                        norm_tile(Tt + 2, R_GMIX, 0, mbuf=Tt % 2, part="rest")
            attn1(NB - 1)
            conv_b(NT - 1)
            conv_c(NT - 1)
            attn2(NB - 1)
            oproj(NT - 1, 0, 4)
            memattn(NT - 1, 0)
            memattn(NT - 1, 1)
            oproj(NT - 1, 4, 8)
            if "ffn2" in stages:
                ffn.pre[1] = load_ffn_group(1, 1)
        mixer.dn = 0

        def rope_prep():
            prow2 = qk_f[:, 8:10, :].rearrange("p a c -> p (a c)")
            angv = qk_f[:, 10:14, :].rearrange("p a c -> p (a c)")
            ang4 = angv.rearrange("p (s b i) -> p s b i", s=2, b=NB)
            P.op("dve", lambda h: h.tensor_copy(out=prow2[0:NB, :], in_=pos_i[:, :]), reads=["pos_i"], writes=["prow2"])
            P.op("pe", lambda h: h.transpose(out=banks[7][:, 0:NB], in_=prow2[0:NB, :], identity=ident_f[0:NB, 0:NB]),
                 reads=["prow2", "ident_f"], writes=[bk(7)])
            P.op("dve", lambda h: h.tensor_copy(out=posf[:, :], in_=banks[7][:, 0:NB]), reads=[bk(7)], writes=["posf"])
            P.op("dve", lambda h: h.tensor_tensor(out=ang4[:, 0, :, :], in0=posf[:, :].unsqueeze(2).to_broadcast([128, NB, 8]),
                                                  in1=invf[:, :].unsqueeze(1).to_broadcast([128, NB, 8]), op=ALU.mult),
                 reads=["posf", "invf"], writes=["ang"])
            P.op("dve", lambda h: h.tensor_scalar_add(out=ang4[:, 1, :, :], in0=ang4[:, 0, :, :], scalar1=0.5 * math.pi),
                 reads=["ang"], writes=["ang"])
            C1 = 6.28125
            C2 = 2.0 * math.pi - C1
            tmpf = sg[:, 0, 0:256]
            tmpi = sg[:, 1, 0:256].bitcast(I32)
            P.op("dve", lambda h: h.tensor_scalar_mul(out=tmpf, in0=angv, scalar1=1.0 / (2.0 * math.pi)),
                 reads=["ang"], writes=[("sg", 0)])
            P.op("dve", lambda h: h.tensor_copy(out=tmpi, in_=tmpf), reads=[("sg", 0)], writes=[("sg", 1)])
            P.op("dve", lambda h: h.tensor_copy(out=tmpf, in_=tmpi), reads=[("sg", 1)], writes=[("sg", 0)])
            P.op("dve", lambda h: h.scalar_tensor_tensor(out=angv, in0=tmpf, scalar=-C1, in1=angv, op0=ALU.mult, op1=ALU.add),
                 reads=[("sg", 0), "ang"], writes=["ang"])
            P.op("dve", lambda h: h.scalar_tensor_tensor(out=angv, in0=tmpf, scalar=-C2, in1=angv, op0=ALU.mult, op1=ALU.add),
                 reads=[("sg", 0), "ang"], writes=["ang"])
            P.op("dve", lambda h: h.tensor_scalar(out=tmpf, in0=angv, scalar1=math.pi, scalar2=-2.0 * math.pi,
                                                   op0=ALU.is_gt, op1=ALU.mult), reads=["ang"], writes=[("sg", 0)])
            P.op("dve", lambda h: h.tensor_tensor(out=angv, in0=angv, in1=tmpf, op=ALU.add), reads=[("sg", 0), "ang"], writes=["ang"])
            P.op("act", lambda h: h.activation(out=cs[:, :, :, :].rearrange("p s b i -> p (s b i)"), in_=angv, func=AF.Sin),
                 reads=["ang", "small"], writes=["cs"])

        def store_tile_512(t):
            for bb in range(4):
                b = t * 4 + bb
                s = b % 2
                for half in range(2):
                    bank = banks[half]
                    for c4 in range(4):
                        c = half * 4 + c4
                        P.op("pe", lambda h, c=c, c4=c4, bank=bank, b=b: h.transpose(
                            out=bank[:, c4 * 128:(c4 + 1) * 128], in_=xT[:, c, b * 128:(b + 1) * 128],
                            identity=ident_f[:, :]),
                            reads=[("xT", c, t), "ident_f"], writes=[bk(half)], inc=(c4 == 3))
                    dst = xstage[:, s, half * 512:(half + 1) * 512]
                    if half == 0:
                        P.op("act", lambda h, dst=dst, bank=bank: h.copy(out=dst, in_=bank[:, :]),
                             reads=[bk(half)], writes=[("xstage", s)])
                    else:
                        P.op("dve", lambda h, dst=dst, bank=bank: h.tensor_copy(out=dst, in_=bank[:, :]),
                             reads=[bk(half)], writes=[("xstage", s)])
                P.dma("sp", "os%d" % s, out_d[b * 128:(b + 1) * 128, :], xstage[:, s, :],
                      reads=[("xstage", s)], writes=[("out", b)])

        if "touch" in DBG:
            for i, d in enumerate([mem_d, wffn_d[0], wffn_d[1], wmix0_d, wmix1_d,
                      g_ffn_d[0], g_ffn_d[1], g_mix_d, g_q_d, g_k_d, sinks_d, w_dw_d, b_dw_d, g_cln_d, b_cln_d,
                      g_mem_d, g_mq_d, g_mk_d, invf_d]):
                P.dma("sp", "touch", st16[0:1, 0, 0:4], d[0:1, 0:4], writes=[("touch", i)])
            P.dma("sp", "touch", pos_i[0:1, 0:4], pos_d[0:1, 0:4], writes=[("touch", 99)])
        load_x_tile(0)
        load_x_tile(1)
        for b in range(NB):
            transpose_x_tile(b)
            if b + 2 < NB:
                load_x_tile(b + 2)
            if "ffn1" in stages and b == 1:
                load_ffn_group(0, 0, after=[("xstage", 0), ("xstage", 1)], part=1)
            if "ffn1" in stages and b == 5:
                load_ffn_group(0, 0, after=[("xstage", 0), ("xstage", 1)], part=2)
            if "ffn1" in stages and b == NB - 3:
                load_ffn_group(0, 1, after=[("xstage", 0), ("xstage", 1)])
            if "ffn1" in stages and b % 4 == 3:
                norm_tile(b // 4, R_GF1, b // 4)

        if "mix" in stages:
            rope_prep()
            for mt in range(2):
                P.dma("sp", "mem%d" % mt, xstage[:, mt, :], mem_d[mt * 128:(mt + 1) * 128, :], writes=[("xstage", mt)])

        if "ffn1" in stages:
            def after1(gi):
                if "mix" in stages:
                    if gi == len(GROUPS) - 2:
                        load_mixer_weights()
                    else:
                        load_mixer_weights2()
                elif "ffn2" in stages:
                    ffn.pre[gi - (len(GROUPS) - 2)] = load_ffn_group(1, gi - (len(GROUPS) - 2))
            if "ffn2" not in stages and "mix" not in stages:
                ffn.final = store_tile_512
            ffn(0, after1)

        if "mix" in stages:
            if "ffn1" not in stages:
                load_mixer_weights()
                load_mixer_weights2()
            mixer()
            if "ffn2" not in stages:
                for t in range(NT):
                    store_tile_512(t)

        if "ffn2" in stages:
            if "ffn1" not in stages and "mix" not in stages:
                ffn.pre[0] = load_ffn_group(1, 0)
                ffn.pre[1] = load_ffn_group(1, 1)
            for t in range(NT):
                norm_tile(t, R_GF2, t, extra_w=(ALL_HTM if t == 0 else ()))
            ffn.final = store_tile_512
            ffn(1, None)

        if "ffn1" not in stages and "ffn2" not in stages and "mix" not in stages:
            for t in range(NT):
                store_tile_512(t)

        P.wait_all("sp", [("out", b) for b in range(NB)])
        P.emit()
    return nc


_INVF = (np.float32(500000.0) ** (-np.arange(8, dtype=np.float32) / np.float32(8))).astype(np.float32)


def _pack_ffn(wg, wu, wd):
    wg3 = wg.reshape(8, 128, DFF)
    wu3 = wu.reshape(8, 128, DFF)
    wd3 = wd.reshape(NFC, 128, D)
    parts = []
    for f0, G in GROUPS:
        cs_ = slice(f0 * 128, (f0 + G) * 128)
        parts.append(wg3[:, :, cs_].transpose(1, 0, 2).reshape(128, -1))
        parts.append(wu3[:, :, cs_].transpose(1, 0, 2).reshape(128, -1))
        parts.append(wd3[f0:f0 + G].transpose(1, 0, 2).reshape(128, -1))
    return np.ascontiguousarray(np.concatenate(parts, axis=1))


def make_in_maps(inputs):
    f = lambda a: np.ascontiguousarray(np.asarray(a))
    w_in = f(inputs["w_in"])[0]
    w_in_r = np.concatenate([w_in[:, 0:768], w_in[:, 1280:1536], w_in[:, 768:1280]], axis=1)
    wmix0 = np.ascontiguousarray(w_in_r.reshape(8, 128, 1536).transpose(1, 0, 2).reshape(128, 12288))
    w_out = f(inputs["w_out"])[0]
    wo = np.empty((128, 8, D), np.float32)
    wo[0:64, 0:4] = w_out[0:256].reshape(4, 64, D).transpose(1, 0, 2)
    wo[64:128, 0:4] = w_out[256:512].reshape(4, 64, D).transpose(1, 0, 2)
    wo[:, 4:8] = w_out[512:1024].reshape(4, 128, D).transpose(1, 0, 2)
    wmkv = f(inputs["w_mem_kv"])[0].reshape(8, 128, 512).transpose(1, 0, 2).reshape(128, 4096)
    wmix1 = np.ascontiguousarray(np.concatenate([wo.reshape(128, 8192), wmkv], axis=1))
    shared = {
        "invfreq": _INVF.reshape(1, 8),
        "g_ffn1": f(inputs["g_ffn1"]).reshape(8, 128), "g_ffn2": f(inputs["g_ffn2"]).reshape(8, 128),
        "wffn1": _pack_ffn(f(inputs["w_ffn1_gate"])[0], f(inputs["w_ffn1_up"])[0], f(inputs["w_ffn1_down"])[0]),
        "wffn2": _pack_ffn(f(inputs["w_ffn2_gate"])[0], f(inputs["w_ffn2_up"])[0], f(inputs["w_ffn2_down"])[0]),
        "wmix0": wmix0, "wmix1": wmix1,
        "g_mix": f(inputs["g_mix"]).reshape(8, 128),
        "g_q": f(inputs["g_q"]).reshape(1, 64), "g_k": f(inputs["g_k"]).reshape(1, 64),
        "sinks": f(inputs["sinks"]).reshape(2, 4), "w_dw": f(inputs["w_dw"])[0],
        "b_dw": f(inputs["b_dw"]).reshape(2, 128), "g_conv_ln": f(inputs["g_conv_ln"]).reshape(2, 128),
        "b_conv_ln": f(inputs["b_conv_ln"]).reshape(2, 128), "g_mem": f(inputs["g_mem"]).reshape(8, 128),
        "g_mq": f(inputs["g_mq"]).reshape(1, 64),
        "g_mk": f(inputs["g_mk"]).reshape(1, 64),
    }
    x = f(inputs["x"])
    mem = f(inputs["mem"])
    pos = f(inputs["positions"]).astype(np.int32)
    maps = []
    for b in range(8):
        m = dict(shared)
        m["x"] = x[b]
        m["mem"] = mem[b]
        m["positions"] = pos[b].reshape(NB, 128)
        maps.append(m)
    return maps


def kernel(**inputs):
    nc = build_nc()
    in_maps = make_in_maps(inputs)
    res = run_bass_kernel_spmd(nc, in_maps, core_ids=list(range(8)))
    return np.stack([r["out"] for r in res.results], axis=0).astype(np.float32)
```
